# Optimizing a Trainium2 kernel written in Bass

```python
import jax, jax.numpy as jnp
from jax import lax
import numpy as np

D_MODEL = 1024
BATCH = 8
SEQ = 4096
DEPTH = 1

HEAD_DIM = 128
N_ATTN_HEADS = D_MODEL // HEAD_DIM
D_ATTN = N_ATTN_HEADS * HEAD_DIM
D_LRU = D_MODEL
N_LRU_BLOCKS = 8
LRU_BLOCK = D_LRU // N_LRU_BLOCKS
CONV_WIDTH = 4
LRU_C = 8.0
D_MIX = D_ATTN + D_LRU
D_PLE = 256
Q_BLOCK = 128
RMS_EPS = 1e-6
D_IN = 4 * D_ATTN + N_ATTN_HEADS + 2 * D_LRU
SPLIT_POINTS = [D_ATTN, 2 * D_ATTN, 3 * D_ATTN, 3 * D_ATTN + N_ATTN_HEADS,
                4 * D_ATTN + N_ATTN_HEADS, 4 * D_ATTN + N_ATTN_HEADS + D_LRU]

kernel_name = "hymba_fox_rglru_sandwich_ple"


def rmsnorm(x, g):
    xf = x.astype(jnp.float32)
    y = xf * lax.rsqrt(jnp.mean(xf * xf, axis=-1, keepdims=True) + RMS_EPS) * g.astype(jnp.float32)
    return y.astype(x.dtype)


def forgetting_attention(q, k, v, f_logit):
    B, S, H, Dh = q.shape
    nblk = S // Q_BLOCK
    scale = HEAD_DIM ** -0.5
    q = q.transpose(0, 2, 1, 3)
    k = k.transpose(0, 2, 1, 3)
    v = v.transpose(0, 2, 1, 3)
    c = jnp.cumsum(jax.nn.log_sigmoid(f_logit.astype(jnp.float32)), axis=1).transpose(0, 2, 1)
    qb = q.reshape(B, H, nblk, Q_BLOCK, Dh).transpose(2, 0, 1, 3, 4)
    cb = c.reshape(B, H, nblk, Q_BLOCK).transpose(2, 0, 1, 3)
    kpos = jnp.arange(S)

    def one_block(args):
        qi, ci, bi = args
        qpos = bi * Q_BLOCK + jnp.arange(Q_BLOCK)
        s = jnp.einsum('bhqd,bhkd->bhqk', qi, k, preferred_element_type=jnp.float32) * scale
        s = s + (ci[:, :, :, None] - c[:, :, None, :])
        s = jnp.where(kpos[None, :] <= qpos[:, None], s, -jnp.inf)
        w = jax.nn.softmax(s, axis=-1)
        return jnp.einsum('bhqk,bhkd->bhqd', w.astype(v.dtype), v)

    ob = lax.map(one_block, (qb, cb, jnp.arange(nblk)))
    return ob.transpose(1, 0, 3, 2, 4).reshape(B, S, H * Dh)


def causal_depthwise_conv(x, w, b):
    y = lax.conv_general_dilated(
        x, w[:, None, :].astype(x.dtype), window_strides=(1,),
        padding=[(CONV_WIDTH - 1, 0)], dimension_numbers=('NWC', 'WIO', 'NWC'),
        feature_group_count=x.shape[-1])
    return y + b


def rg_lru(xc, w_r, b_r, w_i, b_i, lam):
    B, S, _ = xc.shape
    xb = xc.reshape(B, S, N_LRU_BLOCKS, LRU_BLOCK)
    r = jax.nn.sigmoid((jnp.einsum('bsnj,njk->bsnk', xb, w_r).reshape(B, S, D_LRU) + b_r).astype(jnp.float32))
    i = jax.nn.sigmoid((jnp.einsum('bsnj,njk->bsnk', xb, w_i).reshape(B, S, D_LRU) + b_i).astype(jnp.float32))
    log_a = -LRU_C * r * jax.nn.softplus(-lam.astype(jnp.float32))
    a = jnp.exp(log_a)
    u = jnp.sqrt(-jnp.expm1(2.0 * log_a)) * (i * xc.astype(jnp.float32))

    def combine(left, right):
        a_l, b_l = left
        a_r, b_r2 = right
        return a_l * a_r, a_r * b_l + b_r2

    _, h = lax.associative_scan(combine, (a, u), axis=1)
    return h.astype(xc.dtype)


def setup_inputs(seed: int = 0) -> dict:
    key = jax.random.key(seed)
    ks = jax.random.split(key, 24)
    f32 = jnp.float32
    nrm = lambda k, shape, s: jax.random.normal(k, shape, f32) * s
    x = jax.random.normal(ks[0], (BATCH, SEQ, D_MODEL), f32)
    p = jax.random.normal(ks[1], (DEPTH, BATCH, SEQ, D_PLE), f32)
    w_in = nrm(ks[2], (DEPTH, D_MODEL, D_IN), D_MODEL ** -0.5)
    b_f = jnp.linspace(1.0, 6.0, N_ATTN_HEADS, dtype=f32)[None, :] + nrm(ks[3], (DEPTH, N_ATTN_HEADS), 0.1)
    pre_gain = 1.0 + nrm(ks[4], (DEPTH, D_MODEL), 0.05)
    post_gain = 1.0 + nrm(ks[5], (DEPTH, D_MODEL), 0.05)
    conv_w = nrm(ks[6], (DEPTH, CONV_WIDTH, D_LRU), CONV_WIDTH ** -0.5)
    conv_b = nrm(ks[7], (DEPTH, D_LRU), 0.01)
    w_rgate = nrm(ks[8], (DEPTH, N_LRU_BLOCKS, LRU_BLOCK, LRU_BLOCK), LRU_BLOCK ** -0.5)
    b_rgate = nrm(ks[9], (DEPTH, D_LRU), 0.01)
    w_igate = nrm(ks[10], (DEPTH, N_LRU_BLOCKS, LRU_BLOCK, LRU_BLOCK), LRU_BLOCK ** -0.5)
    b_igate = nrm(ks[11], (DEPTH, D_LRU), 0.01)
    a_pow = jax.random.uniform(ks[12], (DEPTH, D_LRU), f32, minval=0.9, maxval=0.999)
    a0 = a_pow ** (1.0 / LRU_C)
    lru_lambda = jnp.log(a0) - jnp.log1p(-a0)
    attn_out_gain = 1.0 + nrm(ks[13], (DEPTH, D_ATTN), 0.05)
    lru_out_gain = 1.0 + nrm(ks[14], (DEPTH, D_LRU), 0.05)
    w_out = nrm(ks[15], (DEPTH, D_MIX, D_MODEL), D_MIX ** -0.5)
    w_ple = nrm(ks[16], (DEPTH, D_PLE, D_MODEL), D_PLE ** -0.5)
    ple_gain = 1.0 + nrm(ks[17], (DEPTH, D_MODEL), 0.05)
    w_ple_gate = nrm(ks[18], (DEPTH, D_MODEL, D_MODEL), D_MODEL ** -0.5)
    b_ple_gate = nrm(ks[19], (DEPTH, D_MODEL), 0.01)
    return {"x": x, "p": p, "w_in": w_in, "b_f": b_f, "pre_gain": pre_gain,
            "post_gain": post_gain, "conv_w": conv_w, "conv_b": conv_b,
            "w_rgate": w_rgate, "b_rgate": b_rgate, "w_igate": w_igate, "b_igate": b_igate,
            "lru_lambda": lru_lambda, "attn_out_gain": attn_out_gain, "lru_out_gain": lru_out_gain,
            "w_out": w_out, "w_ple": w_ple, "ple_gain": ple_gain,
            "w_ple_gate": w_ple_gate, "b_ple_gate": b_ple_gate}


def reference(x, p, w_in, b_f, pre_gain, post_gain, conv_w, conv_b, w_rgate, b_rgate,
              w_igate, b_igate, lru_lambda, attn_out_gain, lru_out_gain, w_out,
              w_ple, ple_gain, w_ple_gate, b_ple_gate):
    B, S, _ = x.shape
    h = x
    for i in range(DEPTH):
        xn = rmsnorm(h, pre_gain[i])
        z = xn @ w_in[i]
        q, k, v, fl, g_attn, x_lru, g_lru = jnp.split(z, SPLIT_POINTS, axis=-1)
        fl = fl + b_f[i]
        o_attn = forgetting_attention(q.reshape(B, S, N_ATTN_HEADS, HEAD_DIM),
                                      k.reshape(B, S, N_ATTN_HEADS, HEAD_DIM),
                                      v.reshape(B, S, N_ATTN_HEADS, HEAD_DIM), fl)
        y_attn = rmsnorm(o_attn, attn_out_gain[i]) * jax.nn.silu(g_attn)
        xc = causal_depthwise_conv(x_lru, conv_w[i], conv_b[i])
        o_lru = rg_lru(xc, w_rgate[i], b_rgate[i], w_igate[i], b_igate[i], lru_lambda[i])
        y_lru = rmsnorm(o_lru, lru_out_gain[i]) * jax.nn.silu(g_lru)
        mix = jnp.concatenate([y_attn, y_lru], axis=-1) @ w_out[i]
        h = h + rmsnorm(mix, post_gain[i])
        e = rmsnorm(p[i] @ w_ple[i], ple_gain[i])
        gate = jax.nn.sigmoid(h @ w_ple_gate[i] + b_ple_gate[i])
        h = h + gate * e
    return h
```

```python
import numpy as np
from contextlib import ExitStack
import concourse.bass as bass
import concourse.mybir as mybir
from concourse.bass_utils import run_bass_kernel_spmd

F32 = mybir.dt.float32
BF16 = mybir.dt.bfloat16
AF = mybir.ActivationFunctionType
ALU = mybir.AluOpType
AX = mybir.AxisListType

S = 4096
D = 1024
DIN = 6152
NT = 32
NCH = 8
OFF_Q, OFF_K, OFF_V, OFF_F, OFF_GA, OFF_XL, OFF_GL = 0, 1024, 2048, 3072, 3080, 4104, 5128
NCOLS = 96
C_PRE, C_AG, C_LG, C_CW, C_CB, C_BR, C_BI, C_LAM = 0, 8, 16, 24, 56, 64, 72, 80
EPS = 1e-6
SCALE = 128 ** -0.5


class Dep:
    __slots__ = ("w", "r", "dsem", "dcnt")

    def __init__(self):
        self.w = None
        self.r = {}
        self.dsem = None
        self.dcnt = 0


class Eng:
    def __init__(self, ctx, name, e):
        self.name = name
        self.e = e
        self.sem = ctx.es.enter_context(ctx.nc.semaphore("s_" + name))
        self.cnt = 0
        self.seen = {}

    def wait(self, tok):
        if tok is None:
            return
        sem, val = tok
        if self.name == "pe" and sem is self.sem:
            return
        k = id(sem)
        if self.seen.get(k, 0) >= val:
            return
        self.e.wait_ge(sem, val)
        self.seen[k] = val

    def signal(self, ins):
        self.cnt += 1
        ins.then_inc(self.sem, 1)
        return (self.sem, self.cnt)

    def pending(self):
        return (self.sem, self.cnt + 1)


class Ctx:
    def __init__(self, nc, es):
        self.nc = nc
        self.es = es
        self.engs = {}
        for name, e in (("pe", nc.tensor), ("act", nc.scalar), ("dve", nc.vector),
                        ("pool", nc.gpsimd), ("sp", nc.sync)):
            self.engs[name] = Eng(self, name, e)
        self.nsem = 0
        self.dma_last = {}

    def barrier(self):
        toks = [(E.sem, E.cnt) for E in self.engs.values() if E.cnt > 0] + list(self.dma_last.values())
        for E in self.engs.values():
            for t in toks:
                E.wait(t)

    def sb(self, es, name, shape, dt):
        return es.enter_context(self.nc.sbuf_tensor("sb_" + name, list(shape), dt))

    def _pre(self, E, reads, writes):
        for d in reads:
            E.wait(d.w)
        for d in writes:
            E.wait(d.w)
            for tok in list(d.r.values()):
                E.wait(tok)

    def _post(self, tok, reads, writes):
        k = id(tok[0])
        for d in reads:
            if k not in d.r or d.r[k][1] < tok[1]:
                d.r[k] = tok
        for d in writes:
            d.w = tok
            d.r = {}

    def op(self, eng, fn, reads=(), writes=(), sig=True):
        E = self.engs[eng]
        self._pre(E, reads, writes)
        ins = fn(E.e)
        tok = E.signal(ins) if sig else E.pending()
        self._post(tok, reads, writes)
        return tok

    def dma(self, q, out, in_, reads=(), writes=(), slot=None):
        E = self.engs[q]
        self._pre(E, reads, writes)
        s = slot if slot is not None else (writes[0] if writes else reads[0])
        if s.dsem is None:
            self.nsem += 1
            s.dsem = self.es.enter_context(self.nc.semaphore("dm%d" % self.nsem))
        s.dcnt += 16
        E.e.dma_start(out=out, in_=in_).then_inc(s.dsem, 16)
        tok = (s.dsem, s.dcnt)
        self.dma_last[id(s.dsem)] = tok
        self._post(tok, reads, writes)
        return tok


def build_nc():
    nc = bass.Bass("TRN2", target_bir_lowering=False)
    x = nc.dram_tensor("x", [S, D], F32, kind="ExternalInput").ap()
    p_in = nc.dram_tensor("p", [S, 256], F32, kind="ExternalInput").ap()
    w_in = nc.dram_tensor("w_in", [D, DIN], F32, kind="ExternalInput").ap()
    w_out = nc.dram_tensor("w_out", [2048, D], F32, kind="ExternalInput").ap()
    w_ple = nc.dram_tensor("w_ple", [256, D], F32, kind="ExternalInput").ap()
    w_pg = nc.dram_tensor("w_pg", [D, D], F32, kind="ExternalInput").ap()
    w_r = nc.dram_tensor("w_r", [8, 128, 128], F32, kind="ExternalInput").ap()
    w_i = nc.dram_tensor("w_i", [8, 128, 128], F32, kind="ExternalInput").ap()
    cols_d = nc.dram_tensor("cols", [128, NCOLS], F32, kind="ExternalInput").ap()
    rows_d = nc.dram_tensor("rows", [3, D], F32, kind="ExternalInput").ap()
    bf_d = nc.dram_tensor("bf", [8, 1], F32, kind="ExternalInput").ap()
    out = nc.dram_tensor("out", [S, D], F32, kind="ExternalOutput").ap()
    ylT_d = nc.dram_tensor("ylT_d", [8, 128, S], BF16, kind="Internal").ap()
    caq_d = nc.dram_tensor("caq_d", [8, 3, S], BF16, kind="Internal").ap()
    cak_d = nc.dram_tensor("cak_d", [8, 3, S], BF16, kind="Internal").ap()

    with ExitStack() as es:
        c = Ctx(nc, es)
        banks = []
        bdeps = []
        for i in range(8):
            banks.append(es.enter_context(nc.psum_tensor("bank%d" % i, [128, 512], F32)))
            bdeps.append(Dep())

        def bank_bf(i):
            return banks[i][:].bitcast(BF16)

        xnT_d = [Dep() for _ in range(NT)]
        yaT = c.sb(es, "yaT", [128, 8, S], BF16)
        yaT_d = [[Dep() for _ in range(NCH)] for _ in range(8)]
        cols = c.sb(es, "cols", [128, NCOLS], F32); cols_dep = Dep()
        ncols = c.sb(es, "ncols", [128, NCOLS], F32); ncols_dep = Dep()
        cvec = c.sb(es, "cvec", [128, 16], F32); cvec_dep = Dep()
        onecol = c.sb(es, "onecol", [128, 1], F32); onecol_dep = Dep()
        epscol = c.sb(es, "epscol", [128, 1], F32); epscol_dep = Dep()
        identf = c.sb(es, "identf", [128, 128], F32); identf_dep = Dep()
        ident = c.sb(es, "ident", [128, 128], BF16); ident_dep = Dep()
        tri = c.sb(es, "tri", [128, 128], BF16); tri_dep = Dep()
        ones = c.sb(es, "ones", [128, 128], BF16); ones_dep = Dep()
        rstd_a = c.sb(es, "rstd_a", [128, NT], F32); rstd_a_dep = Dep()
        rstd_l = c.sb(es, "rstd_l", [128, NT], F32); rstd_l_dep = Dep()
        wst_dep = [Dep() for _ in range(2)]
        wst_i = [0]
        NST = [2]
        CAST_ENGS = ["pool"]
        ylT_dd = [[Dep() for _ in range(NCH)] for _ in range(8)]
        xs = ExitStack()
        wst = [c.sb(xs, "wst%d" % i, [128, 8, 128], F32) for i in range(2)]
        xbig = c.sb(xs, "xbig", [128, 8 * S], BF16)
        xnT = xbig[:].rearrange("p (k s) -> p k s", k=8)
        wo = xbig[:, 0:16 * D].rearrange("p (k n) -> p k n", k=16); wo_dep = Dep()
        wpg = xbig[:, 16 * D:24 * D].rearrange("p (k n) -> p k n", k=8); wpg_dep = Dep()
        wpl = xbig[:, 24 * D:26 * D].rearrange("p (k n) -> p k n", k=2); wpl_dep = Dep()
        rowsb = xbig[:, 26 * D:32 * D].bitcast(F32).rearrange("p (k n) -> p k n", k=3); rowsb_dep = Dep()

        def load_w(dst, dst_dep, src, kc, extra_writes=()):
            i = wst_i[0] % NST[0]
            eng = CAST_ENGS[wst_i[0] % len(CAST_ENGS)]
            wst_i[0] += 1
            ncl = src.shape[1]
            st = wst[i][:, 0:kc, 0:ncl]
            c.dma("sp", st, src.rearrange("(k p) n -> p k n", p=128), writes=[wst_dep[i]])
            if eng == "act":
                c.op("act", lambda e: e.activation(out=dst, in_=st, func=AF.Copy), reads=[wst_dep[i]],
                     writes=[dst_dep] + list(extra_writes))
            else:
                c.op(eng, lambda e: e.tensor_copy(out=dst, in_=st), reads=[wst_dep[i]],
                     writes=[dst_dep] + list(extra_writes))

        def load_w_split(dst, dst_dep, src, kc, extra_writes=(), eng="dve"):
            st_ = {}

            def dma_fn():
                i = wst_i[0] % NST[0]
                wst_i[0] += 1
                st_["i"] = i
                st_["st"] = wst[i][:, 0:kc, 0:src.shape[1]]
                c.dma("sp", st_["st"], src.rearrange("(k p) n -> p k n", p=128), writes=[wst_dep[i]])

            def cast_fn():
                i = st_["i"]
                c.op(eng, lambda e: e.tensor_copy(out=dst, in_=st_["st"]), reads=[wst_dep[i]],
                     writes=[dst_dep] + list(extra_writes))
            return dma_fn, cast_fn

        def rsqrt_mean(e_dst, src, dep_dst, dep_src, tmp, tmp_dep, n=1024.0):
            c.op("act", lambda e: e.activation(out=tmp, in_=src, func=AF.Ln, scale=1.0 / n, bias=epscol[:, 0:1]),
                 reads=[dep_src, epscol_dep], writes=[tmp_dep])
            c.op("act", lambda e: e.activation(out=e_dst, in_=tmp, func=AF.Exp, scale=-0.5),
                 reads=[tmp_dep], writes=[dep_dst])

        c.dma("sp", cols[:], cols_d[:, :], writes=[cols_dep])
        c.op("pool", lambda e: e.memset(onecol[:], 1.0), writes=[onecol_dep])
        c.op("pool", lambda e: e.memset(epscol[:], EPS), writes=[epscol_dep])
        c.op("pool", lambda e: e.memset(ones[:], 1.0), writes=[ones_dep])
        c.op("pool", lambda e: e.memset(identf[:], 1.0), writes=[identf_dep])
        c.op("pool", lambda e: e.affine_select(out=identf[:], in_=identf[:], pattern=[[1, 128]],
                                                compare_op=ALU.is_equal, fill=0.0, base=0,
                                                channel_multiplier=-1), reads=[identf_dep], writes=[identf_dep])
        c.op("dve", lambda e: e.tensor_copy(out=ident[:], in_=identf[:]), reads=[identf_dep], writes=[ident_dep])
        c.op("pool", lambda e: e.memset(identf[:], 1.0), reads=[], writes=[identf_dep])
        c.op("pool", lambda e: e.affine_select(out=identf[:], in_=identf[:], pattern=[[1, 128]],
                                                compare_op=ALU.is_ge, fill=0.0, base=0,
                                                channel_multiplier=-1), reads=[identf_dep], writes=[identf_dep])
        c.op("dve", lambda e: e.tensor_copy(out=tri[:], in_=identf[:]), reads=[identf_dep], writes=[tri_dep])
        c.op("dve", lambda e: e.tensor_scalar(out=ncols[:], in0=cols[:], scalar1=-1.0, scalar2=None,
                                               op0=ALU.mult), reads=[cols_dep], writes=[ncols_dep])
        c.op("act", lambda e: e.activation(out=cvec[:, 0:8], in_=cols[:, C_LAM:C_LAM + 8], func=AF.Exp, scale=-1.0),
             reads=[cols_dep], writes=[cvec_dep])
        c.op("act", lambda e: e.activation(out=cvec[:, 0:8], in_=cvec[:, 0:8], func=AF.Ln, bias=onecol[:, 0:1]),
             reads=[cvec_dep, onecol_dep], writes=[cvec_dep])
        c.op("dve", lambda e: e.tensor_scalar(out=cvec[:, 8:16], in0=cvec[:, 0:8], scalar1=-16.0, scalar2=None,
                                               op0=ALU.mult), reads=[cvec_dep], writes=[cvec_dep])
        c.op("dve", lambda e: e.tensor_scalar(out=cvec[:, 0:8], in0=cvec[:, 0:8], scalar1=-8.0, scalar2=None,
                                               op0=ALU.mult), reads=[cvec_dep], writes=[cvec_dep])

        with ExitStack() as ph:
            xt = [c.sb(ph, "xt%d" % i, [128, D], F32) for i in range(4)]
            xt_dep = [Dep() for _ in range(4)]
            xjunk = c.sb(ph, "xjunk", [128, D], BF16); xjunk_dep = Dep()
            xnb = [c.sb(ph, "xnb%d" % i, [128, D], BF16) for i in range(2)]
            xnb_dep = [Dep() for _ in range(2)]
            ssq0 = c.sb(ph, "ssq0", [128, NT], F32); ssq0_dep = [Dep() for _ in range(NT)]
            rs0 = c.sb(ph, "rs0", [128, NT], F32); rs0_dep = [Dep() for _ in range(NT)]
            tmp0 = c.sb(ph, "tmp0", [128, NT], F32); tmp0_dep = [Dep() for _ in range(NT)]
            def p0_load(tt):
                b = tt % 4
                c.dma(("sp", "pool")[tt % 2], xt[b][:], x[tt * 128:(tt + 1) * 128, :], writes=[xt_dep[b]])

            def p0_a(tt):
                b = tt % 4
                b2 = tt % 2
                c.op("act", lambda e: e.activation(out=xjunk[:], in_=xt[b][:], func=AF.Square,
                                                    accum_out=ssq0[:, tt:tt + 1]),
                     reads=[xt_dep[b]], writes=[xjunk_dep, ssq0_dep[tt]])
                rsqrt_mean(rs0[:, tt:tt + 1], ssq0[:, tt:tt + 1], rs0_dep[tt], ssq0_dep[tt],
                           tmp0[:, tt:tt + 1], tmp0_dep[tt])
                if tt % 2 == 0:
                    c.op("dve", lambda e: e.tensor_scalar(out=xnb[b2][:], in0=xt[b][:], scalar1=rs0[:, tt:tt + 1],
                                                           scalar2=None, op0=ALU.mult),
                         reads=[xt_dep[b], rs0_dep[tt]], writes=[xnb_dep[b2]])
                else:
                    c.op("act", lambda e: e.activation(out=xnb[b2][:], in_=xt[b][:], func=AF.Copy,
                                                        scale=rs0[:, tt:tt + 1]),
                         reads=[xt_dep[b], rs0_dep[tt]], writes=[xnb_dep[b2]])

            def p0_b(tt):
                b2 = tt % 2
                pb = tt % 2
                pv = bank_bf(pb)
                for kc in range(8):
                    c.op("pe", lambda e: e.transpose(out=pv[:, kc * 128:(kc + 1) * 128],
                                                     in_=xnb[b2][:, kc * 128:(kc + 1) * 128], identity=ident[:]),
                         reads=[xnb_dep[b2], ident_dep], writes=[bdeps[pb]], sig=(kc == 7))
                c.op("dve", lambda e: e.tensor_tensor(
                    out=xnT[:, :, tt * 128:(tt + 1) * 128],
                    in0=pv.rearrange("p (k t) -> p k t", k=8),
                    in1=cols[:, C_PRE:C_PRE + 8].unsqueeze(2).to_broadcast([128, 8, 128]),
                    op=ALU.mult), reads=[bdeps[pb], cols_dep], writes=[xnT_d[tt]])

            for tt in range(3):
                p0_load(tt)
            p0_a(0)
            for tt in range(NT):
                if tt + 3 < NT:
                    p0_load(tt + 3)
                if tt + 1 < NT:
                    p0_a(tt + 1)
                p0_b(tt)

        c.barrier()

        def xn_reads(ch):
            return xnT_d[ch * 4:(ch + 1) * 4]

        def proj_chunk(wt, wdep, ch, pbank, M=128):
            for kc in range(8):
                c.op("pe", lambda e: e.matmul(banks[pbank][0:M, :], lhsT=wt[:, kc, 0:M],
                                              rhs=xnT[:, kc, ch * 512:(ch + 1) * 512],
                                              start=(kc == 0), stop=(kc == 7)),
                     reads=[wdep] + xn_reads(ch), writes=[bdeps[pbank]], sig=(kc == 7))

        def act_sigmoid(dst, dst_dep, src, src_deps, nbias=None, nbias_dep=None, P=128):
            if nbias is None:
                c.op("act", lambda e: e.activation(out=dst, in_=src, func=AF.Exp, scale=-1.0),
                     reads=src_deps, writes=[dst_dep])
            else:
                c.op("act", lambda e: e.activation(out=dst, in_=src, func=AF.Exp, scale=-1.0, bias=nbias),
                     reads=src_deps + [nbias_dep], writes=[dst_dep])
            c.op("act", lambda e: e.activation(out=dst, in_=dst, func=AF.Ln, bias=onecol[0:P, 0:1]),
                 reads=[dst_dep, onecol_dep], writes=[dst_dep])
            c.op("act", lambda e: e.activation(out=dst, in_=dst, func=AF.Exp, scale=-1.0),
                 reads=[dst_dep], writes=[dst_dep])

        with ExitStack() as ph:
            wx = [c.sb(ph, "wx%d" % i, [128, 8, 128], BF16) for i in range(2)]; wx_dep = [Dep(), Dep()]
            wg = [c.sb(ph, "wg%d" % i, [128, 8, 128], BF16) for i in range(2)]; wg_dep = [Dep(), Dep()]
            wr = [c.sb(ph, "wr%d" % i, [128, 1, 128], BF16) for i in range(2)]; wr_dep = [Dep(), Dep()]
            wi = [c.sb(ph, "wi%d" % i, [128, 1, 128], BF16) for i in range(2)]; wi_dep = [Dep(), Dep()]
            xl = c.sb(ph, "xl", [128, 3 + S], F32); xl_dep = [Dep() for _ in range(NCH)]; xl0_dep = Dep()

            def two(name, dt=F32):
                return ([c.sb(ph, "%s%d" % (name, i), [128, 512], dt) for i in range(2)], [Dep(), Dep()])
            xc, xc_dep = two("xc")
            xcb, xcb_dep = two("xcb", BF16)
            rr, rr_dep = two("rr")
            ii, ii_dep = two("ii")
            aa, aa_dep = two("aa")
            a2, a2_dep = two("a2")
            uu, uu_dep = two("uu")
            hh, hh_dep = two("hh")
            hsq, hsq_dep = two("hsq", BF16)
            eg, eg_dep = two("eg")
            yl, yl_dep = two("yl", BF16)
            ssql = c.sb(ph, "ssql", [128, NT], F32); ssql_dep = Dep()
            tmpl = c.sb(ph, "tmpl", [128, NT], F32); tmpl_dep = Dep()
            B_SSQ = 7
            rot = [0]

            def nb():
                b_ = rot[0] % 7
                rot[0] += 1
                return b_
            c.op("pool", lambda e: e.memset(xl[:, 0:3], 0.0), writes=[xl0_dep])
            pg_of = {}
            NG = 8 * NCH

            lw_jobs = []

            def lru_weight_jobs(n):
                wb = n % 2
                return [load_w_split(wx[wb][:], wx_dep[wb], w_in[:, OFF_XL + n * 128:OFF_XL + (n + 1) * 128], 8),
                        load_w_split(wg[wb][:], wg_dep[wb], w_in[:, OFF_GL + n * 128:OFF_GL + (n + 1) * 128], 8),
                        load_w_split(wr[wb][:], wr_dep[wb], w_r[n, :, :], 1),
                        load_w_split(wi[wb][:], wi_dep[wb], w_i[n, :, :], 1)]

            def lru_weights(n):
                wb = n % 2
                load_w(wx[wb][:], wx_dep[wb], w_in[:, OFF_XL + n * 128:OFF_XL + (n + 1) * 128], 8)
                load_w(wg[wb][:], wg_dep[wb], w_in[:, OFF_GL + n * 128:OFF_GL + (n + 1) * 128], 8)
                load_w(wr[wb][:], wr_dep[wb], w_r[n, :, :], 1)
                load_w(wi[wb][:], wi_dep[wb], w_i[n, :, :], 1)

            def S1p(g):
                n, ch = divmod(g, NCH)
                wb = n % 2
                b = g % 2
                if n + 1 < 8:
                    if ch == 2:
                        lw_jobs[:] = lru_weight_jobs(n + 1)
                        lw_jobs[0][0]()
                        lw_jobs[1][0]()
                    elif ch == 3:
                        lw_jobs[0][1]()
                        lw_jobs[1][1]()
                        lw_jobs[2][0]()
                        lw_jobs[3][0]()
                    elif ch == 4:
                        lw_jobs[2][1]()
                        lw_jobs[3][1]()
                pb = (0, 1)[g % 2]
                proj_chunk(wx[wb], wx_dep[wb], ch, pb)

            def S1e(g):
                n, ch = divmod(g, NCH)
                wb = n % 2
                b = g % 2
                pb = (0, 1)[g % 2]
                cs = slice(3 + ch * 512, 3 + (ch + 1) * 512)
                c.op("act", lambda e: e.activation(out=xl[:, cs], in_=banks[pb][:], func=AF.Copy),
                     reads=[bdeps[pb]], writes=[xl_dep[ch]])
                prev = [xl_dep[ch - 1]] if ch > 0 else [xl0_dep]
                base = ch * 512
                c.op("dve", lambda e: e.tensor_scalar(out=xc[b][:], in0=xl[:, base:base + 512],
                                                       scalar1=cols[:, C_CW + n:C_CW + n + 1],
                                                       scalar2=cols[:, C_CB + n:C_CB + n + 1],
                                                       op0=ALU.mult, op1=ALU.add),
                     reads=[xl_dep[ch], cols_dep] + prev, writes=[xc_dep[b]])
                for j in range(1, 4):
                    c.op("dve", lambda e: e.scalar_tensor_tensor(
                        out=xc[b][:], in0=xl[:, base + j:base + j + 512],
                        scalar=cols[:, C_CW + 8 * j + n:C_CW + 8 * j + n + 1], in1=xc[b][:],
                        op0=ALU.mult, op1=ALU.add),
                        reads=[xl_dep[ch], cols_dep, xc_dep[b]] + prev, writes=[xc_dep[b]])
                c.op("dve", lambda e: e.tensor_copy(out=xcb[b][:], in_=xc[b][:]), reads=[xc_dep[b]],
                     writes=[xcb_dep[b]])
                pg = (2, 3, 6)[g % 3]
                pg_of[g] = pg
                proj_chunk(wg[wb], wg_dep[wb], ch, pg)

            def S3(g):
                n, ch = divmod(g, NCH)
                wb = n % 2
                b = g % 2
                pr = 4
                pi_ = 5
                c.op("pe", lambda e: e.matmul(banks[pr][:], lhsT=wr[wb][:, 0, :], rhs=xcb[b][:],
                                              start=True, stop=True),
                     reads=[wr_dep[wb], xcb_dep[b]], writes=[bdeps[pr]])
                c.op("pe", lambda e: e.matmul(banks[pi_][:], lhsT=wi[wb][:, 0, :], rhs=xcb[b][:],
                                              start=True, stop=True),
                     reads=[wi_dep[wb], xcb_dep[b]], writes=[bdeps[pi_]])
                c.op("act", lambda e: e.activation(out=rr[b][:], in_=banks[pr][:], func=AF.Exp, scale=-1.0,
                                                    bias=ncols[:, C_BR + n:C_BR + n + 1]),
                     reads=[bdeps[pr], ncols_dep], writes=[rr_dep[b]])
                if g + 1 < NG:
                    S1e(g + 1)
                c.op("act", lambda e: e.activation(out=rr[b][:], in_=rr[b][:], func=AF.Ln, bias=onecol[:, 0:1]),
                     reads=[rr_dep[b], onecol_dep], writes=[rr_dep[b]])
                c.op("act", lambda e: e.activation(out=rr[b][:], in_=rr[b][:], func=AF.Exp, scale=-1.0),
                     reads=[rr_dep[b]], writes=[rr_dep[b]])
                c.op("act", lambda e: e.activation(out=aa[b][:], in_=rr[b][:], func=AF.Exp, scale=cvec[:, n:n + 1]),
                     reads=[rr_dep[b], cvec_dep], writes=[aa_dep[b]])
                c.op("pool", lambda e: e.tensor_tensor(out=a2[b][:], in0=aa[b][:], in1=aa[b][:], op=ALU.mult),
                     reads=[aa_dep[b]], writes=[a2_dep[b]])
                act_sigmoid(ii[b][:], ii_dep[b], banks[pi_][:], [bdeps[pi_]],
                            ncols[:, C_BI + n:C_BI + n + 1], ncols_dep)
                c.op("act", lambda e: e.activation(out=a2[b][:], in_=a2[b][:], func=AF.Ln, scale=-1.0,
                                                    bias=onecol[:, 0:1]),
                     reads=[a2_dep[b], onecol_dep], writes=[a2_dep[b]])
                c.op("act", lambda e: e.activation(out=a2[b][:], in_=a2[b][:], func=AF.Exp, scale=0.5),
                     reads=[a2_dep[b]], writes=[a2_dep[b]])

            def S3b(g):
                n, ch = divmod(g, NCH)
                wb = n % 2
                b = g % 2
                c.op("pool", lambda e: e.tensor_tensor(out=uu[b][:], in0=ii[b][:], in1=xc[b][:], op=ALU.mult),
                     reads=[ii_dep[b], xc_dep[b]], writes=[uu_dep[b]])
                c.op("pool", lambda e: e.tensor_tensor(out=uu[b][:], in0=uu[b][:], in1=a2[b][:], op=ALU.mult),
                     reads=[uu_dep[b], a2_dep[b]], writes=[uu_dep[b]])
                if ch == 0:
                    c.op("dve", lambda e: e.tensor_tensor_scan(out=hh[b][:], data0=aa[b][:], data1=uu[b][:],
                                                                initial=0.0, op0=ALU.mult, op1=ALU.add),
                         reads=[aa_dep[b], uu_dep[b]], writes=[hh_dep[b]])
                else:
                    c.op("dve", lambda e: e.tensor_tensor_scan(out=hh[b][:], data0=aa[b][:], data1=uu[b][:],
                                                                initial=hh[1 - b][:, 511:512],
                                                                op0=ALU.mult, op1=ALU.add),
                         reads=[aa_dep[b], uu_dep[b], hh_dep[1 - b]], writes=[hh_dep[b]])
                c.op("pool", lambda e: e.tensor_tensor(out=hsq[b][:], in0=hh[b][:], in1=hh[b][:], op=ALU.mult),
                     reads=[hh_dep[b]], writes=[hsq_dep[b]])
                pg = pg_of.pop(g)
                act_sigmoid(eg[b][:], eg_dep[b], banks[pg][:], [bdeps[pg]])
                c.op("dve", lambda e: e.tensor_tensor(out=eg[b][:], in0=banks[pg][:], in1=eg[b][:], op=ALU.mult),
                     reads=[bdeps[pg], eg_dep[b]], writes=[eg_dep[b]])
                c.op("dve", lambda e: e.scalar_tensor_tensor(out=yl[b][:], in0=hh[b][:],
                                                              scalar=cols[:, C_LG + n:C_LG + n + 1],
                                                              in1=eg[b][:], op0=ALU.mult, op1=ALU.mult),
                     reads=[hh_dep[b], eg_dep[b], cols_dep], writes=[yl_dep[b]])
                c.dma("sp", ylT_d[n, :, ch * 512:(ch + 1) * 512], yl[b][:], reads=[yl_dep[b]],
                      writes=[ylT_dd[n][ch]], slot=yl_dep[b])

            def SSQ(g):
                n, ch = divmod(g, NCH)
                b = g % 2
                for q in range(4):
                    tt = ch * 4 + q
                    col = n * NT + tt
                    c.op("pe", lambda e: e.matmul(banks[B_SSQ][:, col:col + 1],
                                                  lhsT=hsq[b][:, q * 128:(q + 1) * 128], rhs=ones[:, 0:1],
                                                  start=True, stop=True),
                         reads=[hsq_dep[b], ones_dep], writes=[bdeps[B_SSQ]], sig=(q == 3))

            lru_weights(0)
            S1p(0)
            S1p(1)
            S1e(0)
            for g in range(NG):
                S3(g)
                if g + 2 < NG:
                    S1p(g + 2)
                S3b(g)
                if g >= 1:
                    SSQ(g - 1)
            SSQ(NG - 1)
            c.op("dve", lambda e: e.tensor_reduce(out=ssql[:], in_=banks[B_SSQ][:, 0:256].rearrange(
                "p (n t) -> p t n", n=8), axis=AX.X, op=ALU.add), reads=[bdeps[B_SSQ]], writes=[ssql_dep])
            rsqrt_mean(rstd_l[:], ssql[:], rstd_l_dep, ssql_dep, tmpl[:], tmpl_dep)

        c.barrier()
        with ExitStack() as ph:
            B_SC = ((0, 1), (2, 3))
            B_NUM, B_DEN, B_MISC, B_SSQ = 4, 5, 6, 7
            caq_dd = [Dep() for _ in range(NCH)]
            cak_dd = [Dep() for _ in range(NCH)]
            with ExitStack() as ph2:
                wf = c.sb(ph2, "wf", [128, 8, 8], BF16); wf_dep = Dep()
                nbf = c.sb(ph2, "nbf", [8, 1], F32); nbf_dep = Dep()
                one8 = c.sb(ph2, "one8", [8, 512], F32); one8_dep = Dep()
                lsp = [c.sb(ph2, "lsp%d" % i, [8, 512], F32) for i in range(2)]; lsp_dep = [Dep(), Dep()]
                ncs = [c.sb(ph2, "ncs%d" % i, [8, 512], F32) for i in range(2)]; ncs_dep = [Dep(), Dep()]
                res = c.sb(ph2, "res", [8, 512], F32); res_dep = Dep()
                cpos = [c.sb(ph2, "cpos%d" % i, [8, 3, 512], BF16) for i in range(2)]; cpos_dep = [Dep(), Dep()]
                cneg = [c.sb(ph2, "cneg%d" % i, [8, 3, 512], BF16) for i in range(2)]; cneg_dep = [Dep(), Dep()]
                pass
                load_w(wf[:], wf_dep, w_in[:, OFF_F:OFF_F + 8], 8)
                c.dma("sp", nbf[:], bf_d[:, :], writes=[nbf_dep])
                c.op("dve", lambda e: e.tensor_scalar(out=nbf[:], in0=nbf[:], scalar1=-1.0, scalar2=None,
                                                       op0=ALU.mult), reads=[nbf_dep], writes=[nbf_dep])
                c.op("pool", lambda e: e.memset(one8[:], 1.0), writes=[one8_dep])
                for ch in range(NCH):
                    b = ch % 2
                    proj_chunk(wf, wf_dep, ch, B_MISC, M=8)
                    c.op("act", lambda e: e.activation(out=lsp[b][:], in_=banks[B_MISC][0:8, :], func=AF.Exp,
                                                        scale=-1.0, bias=nbf[:, 0:1]),
                         reads=[bdeps[B_MISC], nbf_dep], writes=[lsp_dep[b]])
                    c.op("act", lambda e: e.activation(out=lsp[b][:], in_=lsp[b][:], func=AF.Ln,
                                                        bias=onecol[0:8, 0:1]),
                         reads=[lsp_dep[b], onecol_dep], writes=[lsp_dep[b]])
                    if ch == 0:
                        c.op("dve", lambda e: e.tensor_tensor_scan(out=ncs[b][:], data0=one8[:], data1=lsp[b][:],
                                                                    initial=0.0, op0=ALU.mult, op1=ALU.add),
                             reads=[one8_dep, lsp_dep[b]], writes=[ncs_dep[b]])
                    else:
                        c.op("dve", lambda e: e.tensor_tensor_scan(out=ncs[b][:], data0=one8[:], data1=lsp[b][:],
                                                                    initial=ncs[1 - b][:, 511:512],
                                                                    op0=ALU.mult, op1=ALU.add),
                             reads=[one8_dep, lsp_dep[b], ncs_dep[1 - b]], writes=[ncs_dep[b]])
                    c.op("act", lambda e: e.activation(out=cpos[b][:, 0, :], in_=ncs[b][:], func=AF.Copy),
                         reads=[ncs_dep[b]], writes=[cpos_dep[b]])
                    c.op("dve", lambda e: e.tensor_tensor(out=res[:], in0=ncs[b][:], in1=cpos[b][:, 0, :],
                                                           op=ALU.subtract),
                         reads=[ncs_dep[b], cpos_dep[b]], writes=[res_dep])
                    c.op("act", lambda e: e.activation(out=cpos[b][:, 1, :], in_=res[:], func=AF.Copy),
                         reads=[res_dep], writes=[cpos_dep[b]])
                    c.op("dve", lambda e: e.tensor_tensor(out=res[:], in0=res[:], in1=cpos[b][:, 1, :],
                                                           op=ALU.subtract),
                         reads=[res_dep, cpos_dep[b]], writes=[res_dep])
                    c.op("act", lambda e: e.activation(out=cpos[b][:, 2, :], in_=res[:], func=AF.Copy),
                         reads=[res_dep], writes=[cpos_dep[b]])
                    c.op("act", lambda e: e.activation(out=cneg[b][:], in_=cpos[b][:], func=AF.Copy, scale=-1.0),
                         reads=[cpos_dep[b]], writes=[cneg_dep[b]])
                    c.dma("sp", caq_d[:, :, ch * 512:(ch + 1) * 512], cneg[b][:], reads=[cneg_dep[b]],
                          writes=[caq_dd[ch]], slot=cneg_dep[b])
                    c.dma("sp", cak_d[:, :, ch * 512:(ch + 1) * 512], cpos[b][:], reads=[cpos_dep[b]],
                          writes=[cak_dd[ch]], slot=cpos_dep[b])

            c.barrier()
            wq = c.sb(ph, "wq", [128, 8, 128], BF16); wq_dep = Dep()
            wk = c.sb(ph, "wk", [128, 8, 128], BF16); wk_dep = Dep()
            wv = c.sb(ph, "wv", [128, 8, 128], BF16); wv_dep = Dep()
            wga = c.sb(ph, "wga", [128, 8, 128], BF16); wga_dep = Dep()
            sgT = c.sb(ph, "sgT", [128, S], BF16); sgT_dep = [Dep() for _ in range(NCH)]
            qT = c.sb(ph, "qT", [128, S], BF16); qT_dep = [Dep() for _ in range(NCH)]
            kT = c.sb(ph, "kT", [128, S], BF16); kT_dep = [Dep() for _ in range(NCH)]
            vv = c.sb(ph, "vv", [128, NT, 128], BF16); vv_dep = [Dep() for _ in range(NCH)]
            augk = c.sb(ph, "augk", [128, S], BF16); augk_dep = Dep()
            augq = [c.sb(ph, "augq%d" % i, [128, 512], BF16) for i in range(2)]; augq_dep = [Dep(), Dep()]
            NE = 6
            EE = [c.sb(ph, "EE%d" % i, [128, 512], BF16) for i in range(NE)]; EE_dep = [Dep() for _ in range(NE)]
            rden = c.sb(ph, "rden", [128, 512], F32); rden_dep = Dep()
            oo = [c.sb(ph, "oo%d" % i, [128, 512], F32) for i in range(2)]; oo_dep = [Dep(), Dep()]
            osq = [c.sb(ph, "osq%d" % i, [128, 512], BF16) for i in range(2)]; osq_dep = [Dep(), Dep()]
            ega = c.sb(ph, "ega", [128, 512], F32); ega_dep = Dep()
            ssqa = c.sb(ph, "ssqa", [128, NT], F32); ssqa_dep = Dep()
            tmpa = c.sb(ph, "tmpa", [128, NT], F32); tmpa_dep = Dep()
            c.op("pool", lambda e: e.memset(augk[:], 0.0), writes=[augk_dep])
            c.op("pool", lambda e: e.memset(augk[0:6, :], 1.0), writes=[augk_dep])
            for i in range(2):
                c.op("pool", lambda e: e.memset(augq[i][:], 0.0), writes=[augq_dep[i]])
                c.op("pool", lambda e: e.memset(augq[i][0:6, :], 1.0), writes=[augq_dep[i]])
            B_SCS = (0, 1, 2, 6)
            B_SSQ = 3
            B_ND = ((4, 5), (7, 5))
            cnt = {"e": 0, "s": 0, "q": 0, "p": 0, "c": 0}
            PROJ_BANKS = (0, 1, 2)

            def head_weight_jobs(h):
                return [load_w_split(wq[:], wq_dep, w_in[:, OFF_Q + h * 128:OFF_Q + (h + 1) * 128], 8),
                        load_w_split(wk[:], wk_dep, w_in[:, OFF_K + h * 128:OFF_K + (h + 1) * 128], 8),
                        load_w_split(wv[:], wv_dep, w_in[:, OFF_V + h * 128:OFF_V + (h + 1) * 128], 8),
                        load_w_split(wga[:], wga_dep, w_in[:, OFF_GA + h * 128:OFF_GA + (h + 1) * 128], 8)]

            def head_inproj(h):
                if h == 0:
                    for dfn, cfn in head_weight_jobs(0):
                        dfn()
                        cfn()
                c.dma("sp", augk[3:6, :], cak_d[h, :, :], reads=cak_dd, writes=[augk_dep])
                for ch in range(NCH):
                    pb = PROJ_BANKS[cnt["p"] % 3]; cnt["p"] += 1
                    proj_chunk(wq, wq_dep, ch, pb)
                    c.op("act", lambda e: e.activation(out=qT[:, ch * 512:(ch + 1) * 512], in_=banks[pb][:],
                                                        func=AF.Copy, scale=SCALE),
                         reads=[bdeps[pb]], writes=[qT_dep[ch]])
                    pb = PROJ_BANKS[cnt["p"] % 3]; cnt["p"] += 1
                    proj_chunk(wk, wk_dep, ch, pb)
                    c.op("dve", lambda e: e.tensor_copy(out=kT[:, ch * 512:(ch + 1) * 512], in_=banks[pb][:]),
                         reads=[bdeps[pb]], writes=[kT_dep[ch]])
                    while carry:
                        carry.pop(0)()
                    pb = PROJ_BANKS[cnt["p"] % 3]; cnt["p"] += 1
                    for q in range(4):
                        tt = ch * 4 + q
                        for kc in range(8):
                            c.op("pe", lambda e: e.matmul(banks[pb][:, q * 128:(q + 1) * 128],
                                                          lhsT=xnT[:, kc, tt * 128:(tt + 1) * 128],
                                                          rhs=wv[:, kc, :], start=(kc == 0), stop=(kc == 7)),
                                 reads=[wv_dep, xnT_d[tt]], writes=[bdeps[pb]], sig=(kc == 7 and q == 3))
                    c.op("act", lambda e: e.activation(
                        out=vv[:, ch * 4:(ch + 1) * 4, :],
                        in_=banks[pb][:].rearrange("p (q d) -> p q d", q=4), func=AF.Copy),
                        reads=[bdeps[pb]], writes=[vv_dep[ch]])
                    pb = PROJ_BANKS[cnt["p"] % 3]; cnt["p"] += 1
                    proj_chunk(wga, wga_dep, ch, pb)
                    act_sigmoid(ega[:], ega_dep, banks[pb][:], [bdeps[pb]])
                    c.op("dve", lambda e: e.scalar_tensor_tensor(out=sgT[:, ch * 512:(ch + 1) * 512],
                                                                  in0=banks[pb][:],
                                                                  scalar=cols[:, C_AG + h:C_AG + h + 1],
                                                                  in1=ega[:], op0=ALU.mult, op1=ALU.mult),
                         reads=[bdeps[pb], ega_dep, cols_dep], writes=[sgT_dep[ch]])

            ES = [c.sb(ph, "ES%d" % i, [128, 512], BF16) for i in range(2)]; ES_dep = [Dep(), Dep()]

            def make_block(h, I, j, n0, ncol, diag, first, last, aq, nd, gpos, glen, gfirst, glast, es, gst):
                st = {}
                bnum, bden = B_ND[nd]

                def qk():
                    bk = B_SCS[cnt["s"] % len(B_SCS)]; cnt["s"] += 1
                    st["bk"] = bk
                    c.op("pe", lambda e: e.matmul(banks[bk][:, 0:ncol], lhsT=kT[:, j * 128:(j + 1) * 128],
                                                  rhs=qT[:, I * 512 + n0:(I + 1) * 512],
                                                  start=True, stop=False),
                         reads=[kT_dep[j // 4], qT_dep[I]], writes=[bdeps[bk]], sig=False)
                    c.op("pe", lambda e: e.matmul(banks[bk][:, 0:ncol], lhsT=augk[:, j * 128:(j + 1) * 128],
                                                  rhs=augq[aq][:, n0:512], start=False, stop=True),
                         reads=[augk_dep, augq_dep[aq]], writes=[bdeps[bk]], sig=True)

                def ex():
                    bk = st["bk"]
                    eb = cnt["e"] % NE; cnt["e"] += 1
                    st["eb"] = eb
                    c.op("act", lambda e: e.activation(out=EE[eb][:, 0:ncol], in_=banks[bk][:, 0:ncol],
                                                        func=AF.Exp),
                         reads=[bdeps[bk]], writes=[EE_dep[eb]])
                    if diag:
                        c.op("pool", lambda e: e.tensor_tensor(out=EE[eb][:, 0:128], in0=EE[eb][:, 0:128],
                                                                in1=tri[:], op=ALU.mult),
                             reads=[EE_dep[eb], tri_dep], writes=[EE_dep[eb]])
                    if glen > 1:
                        if gpos == 0:
                            gst["eb0"] = eb
                            gst["n00"] = n0
                        elif gpos == 1:
                            eb0 = gst["eb0"]
                            d0 = n0 - gst["n00"]
                            if d0 > 0:
                                c.op("pool", lambda e: e.tensor_copy(out=ES[es][:, gst["n00"]:n0],
                                                                      in_=EE[eb0][:, 0:d0]),
                                     reads=[EE_dep[eb0]], writes=[ES_dep[es]])
                            c.op("dve", lambda e: e.tensor_tensor(out=ES[es][:, n0:n0 + ncol],
                                                                   in0=EE[eb0][:, d0:d0 + ncol],
                                                                   in1=EE[eb][:, 0:ncol], op=ALU.add),
                                 reads=[EE_dep[eb0], EE_dep[eb]], writes=[ES_dep[es]])
                        else:
                            c.op("dve", lambda e: e.tensor_tensor(out=ES[es][:, n0:n0 + ncol],
                                                                   in0=ES[es][:, n0:n0 + ncol],
                                                                   in1=EE[eb][:, 0:ncol], op=ALU.add),
                                 reads=[ES_dep[es], EE_dep[eb]], writes=[ES_dep[es]])

                def pv():
                    eb = st["eb"]
                    has_den = (glen == 1) or (gpos == glen - 1)
                    c.op("pe", lambda e: e.matmul(banks[bnum][:, n0:n0 + ncol], lhsT=vv[:, j, :],
                                                  rhs=EE[eb][:, 0:ncol], start=first, stop=last),
                         reads=[vv_dep[j // 4], EE_dep[eb]], writes=[bdeps[bnum]], sig=(not has_den))
                    if glen == 1:
                        c.op("pe", lambda e: e.matmul(banks[bden][:, n0:n0 + ncol], lhsT=ones[:],
                                                      rhs=EE[eb][:, 0:ncol], start=gfirst, stop=glast),
                             reads=[ones_dep, EE_dep[eb]], writes=[bdeps[bden]], sig=True)
                    elif gpos == glen - 1:
                        g0 = gst["n00"]
                        c.op("pe", lambda e: e.matmul(banks[bden][:, g0:512], lhsT=ones[:],
                                                      rhs=ES[es][:, g0:512], start=gfirst, stop=glast),
                             reads=[ones_dep, ES_dep[es]], writes=[bdeps[bden]], sig=True)
                return qk, ex, pv

            def make_pre(h, I, aq):
                def pre():
                    c.dma("sp", augq[aq][0:3, :], caq_d[h, :, I * 512:(I + 1) * 512], reads=[caq_dd[I]],
                          writes=[augq_dep[aq]])
                return pre

            def make_final(h, I, nd):
                bnum, bden = B_ND[nd]
                ob = I % 2
                qs = slice(I * 512, (I + 1) * 512)

                def fin():
                    c.op("act", lambda e: e.activation(out=rden[:], in_=banks[bden][:], func=AF.Ln),
                         reads=[bdeps[bden]], writes=[rden_dep])
                    c.op("act", lambda e: e.activation(out=rden[:], in_=rden[:], func=AF.Exp, scale=-1.0),
                         reads=[rden_dep], writes=[rden_dep])
                    c.op("dve", lambda e: e.tensor_tensor(out=oo[ob][:], in0=banks[bnum][:], in1=rden[:],
                                                           op=ALU.mult),
                         reads=[bdeps[bnum], rden_dep], writes=[oo_dep[ob]])
                    c.op("dve", lambda e: e.tensor_tensor(out=osq[ob][:], in0=oo[ob][:], in1=oo[ob][:],
                                                           op=ALU.mult),
                         reads=[oo_dep[ob]], writes=[osq_dep[ob]])
                    c.op("dve", lambda e: e.tensor_tensor(out=yaT[:, h, qs], in0=oo[ob][:], in1=sgT[:, qs],
                                                           op=ALU.mult),
                         reads=[oo_dep[ob], sgT_dep[I]], writes=[yaT_d[h][I]])

                def fin_pe():
                    for q in range(4):
                        col = h * NT + I * 4 + q
                        c.op("pe", lambda e: e.matmul(banks[B_SSQ][:, col:col + 1],
                                                      lhsT=osq[ob][:, q * 128:(q + 1) * 128], rhs=ones[:, 0:1],
                                                      start=True, stop=True),
                             reads=[osq_dep[ob], ones_dep], writes=[bdeps[B_SSQ]], sig=(q == 3))
                return fin, fin_pe

            pre_jobs = []
            for g_ in range(8):
                for kc2 in range(2):
                    pre_jobs.append(load_w_split(
                        wo[:, kc2 * 8:(kc2 + 1) * 8, g_ * 128:(g_ + 1) * 128], wo_dep,
                        w_out[kc2 * 1024:(kc2 + 1) * 1024, g_ * 128:(g_ + 1) * 128], 8, extra_writes=xnT_d))
                pre_jobs.append(load_w_split(wpg[:, :, g_ * 128:(g_ + 1) * 128], wpg_dep,
                                             w_pg[:, g_ * 128:(g_ + 1) * 128], 8, extra_writes=xnT_d))
                pre_jobs.append(load_w_split(wpl[:, :, g_ * 128:(g_ + 1) * 128], wpl_dep,
                                             w_ple[:, g_ * 128:(g_ + 1) * 128], 2, extra_writes=xnT_d))
            pend_casts = []
            carry = []
            for h in range(8):
                head_inproj(h)
                hw_jobs = head_weight_jobs(h + 1) if h < 7 else []
                tasks = []
                make_pre(h, 0, cnt["q"] % 2)()
                for I in range(NCH):
                    aq = cnt["q"] % 2; cnt["q"] += 1
                    nd = cnt["c"] % 2; cnt["c"] += 1
                    blks = [(j, 0, 512, False) for j in range(4 * I)]
                    blks += [(4 * I + r, 128 * r, 512 - 128 * r, True) for r in range(4)]
                    fin, fin_pe = make_final(h, I, nd)
                    ngrp = len(blks) // 4
                    gsts = [dict() for _ in range(ngrp)]
                    for bi, (j, n0, ncol, diag) in enumerate(blks):
                        gi_, gpos = divmod(bi, 4)
                        if gpos == 0:
                            cnt["g"] = cnt.get("g", 0) + 1
                        qk, ex, pv = make_block(h, I, j, n0, ncol, diag, bi == 0, bi == len(blks) - 1, aq, nd,
                                                gpos, 4, gi_ == 0, gi_ == ngrp - 1, cnt["g"] % 2, gsts[gi_])
                        tasks.append({"pre": make_pre(h, I + 1, 1 - aq) if (bi == 0 and I + 1 < NCH) else None,
                                      "qk": qk, "ex": ex,
                                      "pv": pv, "fin": fin if bi == len(blks) - 1 else None,
                                      "fin_pe": fin_pe if bi == len(blks) - 1 else None})
                deferred = []

                def front(t):
                    if t["pre"] is not None:
                        t["pre"]()
                    t["qk"]()
                    t["ex"]()
                SK = 4
                for k_ in range(SK):
                    front(tasks[k_])
                for ti, t in enumerate(tasks):
                    if ti + SK < len(tasks):
                        front(tasks[ti + SK])
                    t["pv"]()
                    if h < 7:
                        if ti in (20, 28, 36, 44):
                            dfn, cfn = hw_jobs.pop(0)
                            dfn()
                            pend_casts.append((ti + 6, cfn))
                        while pend_casts and pend_casts[0][0] <= ti:
                            pend_casts.pop(0)[1]()
                    if h == 7:
                        if ti % 4 == 3 and pre_jobs:
                            dfn, cfn = pre_jobs.pop(0)
                            dfn()
                            pend_casts.append((ti + 6, cfn))
                        while pend_casts and pend_casts[0][0] <= ti:
                            pend_casts.pop(0)[1]()
                    due = [d for d in deferred if d[0] <= ti]
                    deferred = [d for d in deferred if d[0] > ti]
                    for d in due:
                        d[1]()
                    if t["fin"] is not None:
                        t["fin"]()
                        deferred.append((ti + 7, t["fin_pe"]))
                carry[:] = [d[1] for d in deferred]
            while carry:
                carry.pop(0)()
            while pend_casts:
                pend_casts.pop(0)[1]()
            while pre_jobs:
                dfn, cfn = pre_jobs.pop(0)
                dfn()
                cfn()
            for i_ in range(3):
                c.dma("sp", rowsb[:, i_, :], rows_d[i_:i_ + 1, :].partition_broadcast(128),
                      writes=[rowsb_dep] + xnT_d)
            c.op("dve", lambda e: e.tensor_reduce(out=ssqa[:], in_=banks[B_SSQ][:, 0:256].rearrange(
                "p (n t) -> p t n", n=8), axis=AX.X, op=ALU.add), reads=[bdeps[B_SSQ]], writes=[ssqa_dep])
            rsqrt_mean(rstd_a[:], ssqa[:], rstd_a_dep, ssqa_dep, tmpa[:], tmpa_dep)

        c.barrier()
        out_toks = []
        with ExitStack() as ph:
            ylc = [c.sb(ph, "ylc%d" % i, [128, 8, 512], BF16) for i in range(2)]; ylc_dep = [Dep(), Dep()]
            xt = [c.sb(ph, "xt3_%d" % i, [128, D], F32) for i in range(2)]; xt_dep = [Dep(), Dep()]
            pt = [c.sb(ph, "pt%d" % i, [128, 256], F32) for i in range(2)]; pt_dep = [Dep(), Dep()]
            ptb = c.sb(ph, "ptb", [128, 256], BF16); ptb_dep = Dep()
            pT = [c.sb(ph, "pT%d" % i, [128, 2, 128], BF16) for i in range(2)]; pT_dep = [Dep(), Dep()]
            t1 = c.sb(ph, "t1", [128, D], F32); t1_dep = Dep()
            mix = c.sb(ph, "mix", [128, D], F32); mix_dep = Dep()
            h1 = [c.sb(ph, "h1_%d" % i, [128, D], F32) for i in range(2)]; h1_dep = [Dep(), Dep()]
            h1b = c.sb(ph, "h1b", [128, D], BF16); h1b_dep = Dep()
            h1T = [c.sb(ph, "h1T%d" % i, [128, 8, 128], BF16) for i in range(2)]; h1T_dep = [Dep(), Dep()]
            gz = c.sb(ph, "gz", [128, D], F32); gz_dep = Dep()
            ee1 = c.sb(ph, "ee", [128, D], F32); ee = [ee1, ee1]; ee_d1 = Dep(); ee_dep = [ee_d1, ee_d1]
            ot = [c.sb(ph, "ot%d" % i, [128, D], F32) for i in range(2)]; ot_dep = [Dep(), Dep()]
            sm = [c.sb(ph, "sm%d" % i, [128, 8], F32) for i in range(2)]
            sm_dep = [[Dep() for _ in range(8)] for _ in range(2)]
            B_TR = 4
            B_GATE = (5, 6)
            B_E = (7, 4)

            def load_ylc(ch_):
                cb_ = ch_ % 2
                for n in range(8):
                    c.dma("sp", ylc[cb_][:, n, :], ylT_d[n, :, ch_ * 512:(ch_ + 1) * 512],
                          reads=[ylT_dd[n][ch_]], writes=[ylc_dep[cb_]])

            def AL(tt):
                b = tt % 2
                ch = tt // 4
                cb = ch % 2
                q = tt % 4
                if q == 0 and ch + 1 < NCH:
                    load_ylc(ch + 1)
                c.dma("sp", xt[b][:], x[tt * 128:(tt + 1) * 128, :], writes=[xt_dep[b]])
                c.dma("sp", pt[b][:], p_in[tt * 128:(tt + 1) * 128, :], writes=[pt_dep[b]])
                ts_ = slice(tt * 128, (tt + 1) * 128)
                for half in range(2):
                    for kc in range(8):
                        c.op("pe", lambda e: e.matmul(banks[half][:], lhsT=yaT[:, kc, ts_],
                                                      rhs=wo[:, kc, half * 512:(half + 1) * 512],
                                                      start=(kc == 0), stop=(kc == 7)),
                             reads=[yaT_d[kc][ch], wo_dep], writes=[bdeps[half]], sig=(kc == 7))
                for half in range(2):
                    for kc in range(8):
                        c.op("pe", lambda e: e.matmul(banks[2 + half][:], lhsT=ylc[cb][:, kc, q * 128:(q + 1) * 128],
                                                      rhs=wo[:, 8 + kc, half * 512:(half + 1) * 512],
                                                      start=(kc == 0), stop=(kc == 7)),
                             reads=[ylc_dep[cb], wo_dep], writes=[bdeps[2 + half]], sig=(kc == 7))

            def rsq_ops(dst, src, dep_dst, dep_src, tmp, tmp_dep, n=1024.0):
                return [
                    lambda: c.op("act", lambda e: e.activation(out=tmp, in_=src, func=AF.Ln, scale=1.0 / n,
                                                                bias=epscol[:, 0:1]),
                                 reads=[dep_src, epscol_dep], writes=[tmp_dep]),
                    lambda: c.op("act", lambda e: e.activation(out=dst, in_=tmp, func=AF.Exp, scale=-0.5),
                                 reads=[tmp_dep], writes=[dep_dst]),
                ]

            def chain_head(tt):
                b = tt % 2
                c.op("pool", lambda e: e.tensor_copy(out=ptb[:], in_=pt[b][:]), reads=[pt_dep[b]], writes=[ptb_dep])
                for half in range(2):
                    hs = slice(half * 512, (half + 1) * 512)
                    c.op("act", lambda e: e.activation(out=t1[:, hs], in_=banks[half][:], func=AF.Copy,
                                                        scale=rstd_a[:, tt:tt + 1]),
                         reads=[bdeps[half], rstd_a_dep], writes=[t1_dep])
                    c.op("dve", lambda e: e.scalar_tensor_tensor(out=mix[:, hs], in0=banks[2 + half][:],
                                                                  scalar=rstd_l[:, tt:tt + 1], in1=t1[:, hs],
                                                                  op0=ALU.mult, op1=ALU.add),
                         reads=[bdeps[2 + half], rstd_l_dep, t1_dep], writes=[mix_dep])

            def chain_ops(tt):
                b = tt % 2
                ops = [lambda: c.op("act", lambda e: e.activation(out=h1b[:], in_=mix[:], func=AF.Square,
                                                                   accum_out=sm[b][:, 0:1]),
                                    reads=[mix_dep], writes=[h1b_dep, sm_dep[b][0]])]
                ops += rsq_ops(sm[b][:, 1:2], sm[b][:, 0:1], sm_dep[b][1], sm_dep[b][0], sm[b][:, 2:3], sm_dep[b][2])
                ops += [
                    lambda: c.op("dve", lambda e: e.scalar_tensor_tensor(out=h1[b][:], in0=mix[:],
                                                                          scalar=sm[b][:, 1:2], in1=rowsb[:, 0, :],
                                                                          op0=ALU.mult, op1=ALU.mult),
                                 reads=[mix_dep, sm_dep[b][1], rowsb_dep], writes=[h1_dep[b]]),
                    lambda: c.op("dve", lambda e: e.tensor_tensor(out=h1b[:], in0=h1[b][:], in1=xt[b][:],
                                                                   op=ALU.add),
                                 reads=[h1_dep[b], xt_dep[b]], writes=[h1b_dep]),
                    lambda: c.op("dve", lambda e: e.tensor_tensor(out=h1[b][:], in0=h1[b][:], in1=xt[b][:],
                                                                   op=ALU.add),
                                 reads=[h1_dep[b], xt_dep[b]], writes=[h1_dep[b]]),
                ]
                return ops

            def TRp_ops(tt):
                b = tt % 2
                pv = bank_bf(B_TR)

                def tr_p():
                    for kc in range(2):
                        c.op("pe", lambda e: e.transpose(out=pv[:, kc * 128:(kc + 1) * 128],
                                                         in_=ptb[:, kc * 128:(kc + 1) * 128], identity=ident[:]),
                             reads=[ptb_dep, ident_dep], writes=[bdeps[B_TR]], sig=(kc == 1))
                    c.op("act", lambda e: e.activation(out=pT[b][:],
                                                        in_=pv[:, 0:256].rearrange("p (k t) -> p k t", k=2),
                                                        func=AF.Copy), reads=[bdeps[B_TR]], writes=[pT_dep[b]])

                def e_mm():
                    for half in range(2):
                        bk = B_E[half]
                        for kc in range(2):
                            c.op("pe", lambda e: e.matmul(banks[bk][:], lhsT=pT[b][:, kc, :],
                                                          rhs=wpl[:, kc, half * 512:(half + 1) * 512],
                                                          start=(kc == 0), stop=(kc == 1)),
                                 reads=[pT_dep[b], wpl_dep], writes=[bdeps[bk]], sig=(kc == 1))
                ops = [tr_p, e_mm]
                for half in range(2):
                    ops.append(lambda half=half: c.op(
                        "act", lambda e: e.activation(out=t1[:, half * 512:(half + 1) * 512],
                                                      in_=banks[B_E[half]][:], func=AF.Square,
                                                      accum_out=sm[b][:, 3 + half:4 + half]),
                        reads=[bdeps[B_E[half]]], writes=[t1_dep, sm_dep[b][3 + half]]))
                ops.append(lambda: c.op("dve", lambda e: e.tensor_tensor(out=sm[b][:, 5:6], in0=sm[b][:, 3:4],
                                                                          in1=sm[b][:, 4:5], op=ALU.add),
                                        reads=[sm_dep[b][3], sm_dep[b][4]], writes=[sm_dep[b][5]]))
                ops += rsq_ops(sm[b][:, 6:7], sm[b][:, 5:6], sm_dep[b][6], sm_dep[b][5], sm[b][:, 7:8], sm_dep[b][7])
                for half in range(2):
                    ops.append(lambda half=half: c.op(
                        "dve", lambda e: e.scalar_tensor_tensor(out=ee[b][:, half * 512:(half + 1) * 512],
                                                                in0=banks[B_E[half]][:], scalar=sm[b][:, 6:7],
                                                                in1=rowsb[:, 1, half * 512:(half + 1) * 512],
                                                                op0=ALU.mult, op1=ALU.mult),
                        reads=[bdeps[B_E[half]], sm_dep[b][6], rowsb_dep], writes=[ee_dep[b]]))
                return ops

            def TRh(tt):
                b = tt % 2
                pv = bank_bf(B_TR)
                for kc in range(8):
                    c.op("pe", lambda e: e.transpose(out=pv[:, kc * 128:(kc + 1) * 128],
                                                     in_=h1b[:, kc * 128:(kc + 1) * 128], identity=ident[:]),
                         reads=[h1b_dep, ident_dep], writes=[bdeps[B_TR]], sig=(kc == 7))
                c.op("act", lambda e: e.activation(out=h1T[b][:], in_=pv.rearrange("p (k t) -> p k t", k=8),
                                                    func=AF.Copy), reads=[bdeps[B_TR]], writes=[h1T_dep[b]])

            def Bpe(tt):
                b = tt % 2
                for half in range(2):
                    bk = B_GATE[half]
                    for kc in range(8):
                        c.op("pe", lambda e: e.matmul(banks[bk][:], lhsT=h1T[b][:, kc, :],
                                                      rhs=wpg[:, kc, half * 512:(half + 1) * 512],
                                                      start=(kc == 0), stop=(kc == 7)),
                             reads=[h1T_dep[b], wpg_dep], writes=[bdeps[bk]], sig=(kc == 7))

            def B2_ops(tt):
                b = tt % 2
                ops = []
                for half in range(2):
                    ops.append(lambda half=half: c.op(
                        "dve", lambda e: e.tensor_tensor(out=gz[:, half * 512:(half + 1) * 512],
                                                         in0=banks[B_GATE[half]][:],
                                                         in1=rowsb[:, 2, half * 512:(half + 1) * 512], op=ALU.add),
                        reads=[bdeps[B_GATE[half]], rowsb_dep], writes=[gz_dep]))
                ops += [
                    lambda: c.op("act", lambda e: e.activation(out=gz[:], in_=gz[:], func=AF.Exp, scale=-1.0),
                                 reads=[gz_dep], writes=[gz_dep]),
                    lambda: c.op("act", lambda e: e.activation(out=gz[:], in_=gz[:], func=AF.Ln,
                                                                bias=onecol[:, 0:1]),
                                 reads=[gz_dep, onecol_dep], writes=[gz_dep]),
                    lambda: c.op("act", lambda e: e.activation(out=gz[:], in_=gz[:], func=AF.Exp, scale=-1.0),
                                 reads=[gz_dep], writes=[gz_dep]),
                    lambda: c.op("dve", lambda e: e.tensor_tensor(out=gz[:], in0=ee[b][:], in1=gz[:], op=ALU.mult),
                                 reads=[ee_dep[b], gz_dep], writes=[gz_dep]),
                    lambda: c.op("dve", lambda e: e.tensor_tensor(out=ot[b][:], in0=gz[:], in1=h1[b][:], op=ALU.add),
                                 reads=[gz_dep, h1_dep[b]], writes=[ot_dep[b]]),
                    lambda: out_toks.append(c.dma("pool", out[tt * 128:(tt + 1) * 128, :], ot[b][:],
                                                  reads=[ot_dep[b]], slot=ot_dep[b])),
                ]
                return ops

            def interleave(a, b_):
                n_ = max(len(a), len(b_))
                for k_ in range(n_):
                    if k_ < len(a):
                        a[k_]()
                    if k_ < len(b_):
                        b_[k_]()

            load_ylc(0)
            AL(0)
            chain_head(0)
            trp0 = TRp_ops(0)
            interleave(chain_ops(0), trp0[:1])
            TRh(0)
            for op_ in trp0[1:]:
                op_()
            AL(1)
            for tt in range(NT):
                Bpe(tt)
                if tt + 1 < NT:
                    chain_head(tt + 1)
                if tt + 2 < NT:
                    AL(tt + 2)
                trp = TRp_ops(tt + 1) if tt + 1 < NT else []
                seq_b = B2_ops(tt) + trp[:1]
                seq_a = chain_ops(tt + 1) if tt + 1 < NT else []
                interleave(seq_a, seq_b)
                if tt + 1 < NT:
                    TRh(tt + 1)
                for op_ in trp[1:]:
                    op_()
        xs.close()
        E = c.engs["sp"]
        for t in out_toks[-2:]:
            E.wait(t)
        for nm in ("act", "dve", "pool", "pe"):
            pass
    return nc


_NC_CACHE = {}


def _pack_cols(pre_gain, attn_out_gain, lru_out_gain, conv_w, conv_b, b_rgate, b_igate, lru_lambda):
    def colz(v):
        return np.ascontiguousarray(v.reshape(8, 128).T)
    cols = np.zeros((128, NCOLS), np.float32)
    cols[:, C_PRE:C_PRE + 8] = colz(pre_gain)
    cols[:, C_AG:C_AG + 8] = colz(attn_out_gain)
    cols[:, C_LG:C_LG + 8] = colz(lru_out_gain)
    for j in range(4):
        cols[:, C_CW + 8 * j:C_CW + 8 * j + 8] = colz(conv_w[j])
    cols[:, C_CB:C_CB + 8] = colz(conv_b)
    cols[:, C_BR:C_BR + 8] = colz(b_rgate)
    cols[:, C_BI:C_BI + 8] = colz(b_igate)
    cols[:, C_LAM:C_LAM + 8] = colz(lru_lambda)
    return cols


def kernel(x, p, w_in, b_f, pre_gain, post_gain, conv_w, conv_b, w_rgate, b_rgate,
           w_igate, b_igate, lru_lambda, attn_out_gain, lru_out_gain, w_out,
           w_ple, ple_gain, w_ple_gate, b_ple_gate):
    f = lambda a: np.ascontiguousarray(np.asarray(a, dtype=np.float32))
    x = f(x); p = f(p)
    cols = _pack_cols(f(pre_gain)[0], f(attn_out_gain)[0], f(lru_out_gain)[0], f(conv_w)[0], f(conv_b)[0],
                      f(b_rgate)[0], f(b_igate)[0], f(lru_lambda)[0])
    rows = np.ascontiguousarray(np.stack([f(post_gain)[0], f(ple_gain)[0], f(b_ple_gate)[0]], axis=0))
    bf = np.ascontiguousarray(f(b_f)[0].reshape(8, 1))
    shared = {"w_in": f(w_in)[0], "w_out": f(w_out)[0], "w_ple": f(w_ple)[0], "w_pg": f(w_ple_gate)[0],
              "w_r": f(w_rgate)[0], "w_i": f(w_igate)[0], "cols": cols, "rows": rows, "bf": bf}
    if "nc" not in _NC_CACHE:
        _NC_CACHE["nc"] = build_nc()
    nc = _NC_CACHE["nc"]
    in_maps = []
    for b in range(8):
        m = dict(shared)
        m["x"] = x[b]
        m["p"] = p[0, b]
        in_maps.append(m)
    res = run_bass_kernel_spmd(nc, in_maps, core_ids=list(range(8)))
    return np.stack([np.asarray(r["out"], dtype=np.float32) for r in res.results], axis=0)
```

```python
import numpy as np
from contextlib import ExitStack
import concourse.bass as bass
import concourse.mybir as mybir
from concourse.bass_utils import run_bass_kernel_spmd

F32 = mybir.dt.float32
BF16 = mybir.dt.bfloat16
AF = mybir.ActivationFunctionType
ALU = mybir.AluOpType
AX = mybir.AxisListType

S = 4096
D = 1024
DIN = 6152
NT = 32
NCH = 8
OFF_Q, OFF_K, OFF_V, OFF_F, OFF_GA, OFF_XL, OFF_GL = 0, 1024, 2048, 3072, 3080, 4104, 5128
NCOLS = 96
C_PRE, C_AG, C_LG, C_CW, C_CB, C_BR, C_BI, C_LAM = 0, 8, 16, 24, 56, 64, 72, 80
EPS = 1e-6
SCALE = 128 ** -0.5


class Dep:
    __slots__ = ("w", "r", "dsem", "dcnt")

    def __init__(self):
        self.w = None
        self.r = {}
        self.dsem = None
        self.dcnt = 0


class Eng:
    def __init__(self, ctx, name, e):
        self.name = name
        self.e = e
        self.sem = ctx.es.enter_context(ctx.nc.semaphore("s_" + name))
        self.cnt = 0
        self.seen = {}

    def wait(self, tok):
        if tok is None:
            return
        sem, val = tok
        if self.name == "pe" and sem is self.sem:
            return
        k = id(sem)
        if self.seen.get(k, 0) >= val:
            return
        self.e.wait_ge(sem, val)
        self.seen[k] = val

    def signal(self, ins):
        self.cnt += 1
        ins.then_inc(self.sem, 1)
        return (self.sem, self.cnt)

    def pending(self):
        return (self.sem, self.cnt + 1)


class Ctx:
    def __init__(self, nc, es):
        self.nc = nc
        self.es = es
        self.engs = {}
        for name, e in (("pe", nc.tensor), ("act", nc.scalar), ("dve", nc.vector),
                        ("pool", nc.gpsimd), ("sp", nc.sync)):
            self.engs[name] = Eng(self, name, e)
        self.nsem = 0
        self.dma_last = {}

    def barrier(self):
        toks = [(E.sem, E.cnt) for E in self.engs.values() if E.cnt > 0] + list(self.dma_last.values())
        for E in self.engs.values():
            for t in toks:
                E.wait(t)

    def sb(self, es, name, shape, dt):
        return es.enter_context(self.nc.sbuf_tensor("sb_" + name, list(shape), dt))

    def _pre(self, E, reads, writes):
        for d in reads:
            E.wait(d.w)
        for d in writes:
            E.wait(d.w)
            for tok in list(d.r.values()):
                E.wait(tok)

    def _post(self, tok, reads, writes):
        k = id(tok[0])
        for d in reads:
            if k not in d.r or d.r[k][1] < tok[1]:
                d.r[k] = tok
        for d in writes:
            d.w = tok
            d.r = {}

    def op(self, eng, fn, reads=(), writes=(), sig=True):
        E = self.engs[eng]
        self._pre(E, reads, writes)
        ins = fn(E.e)
        tok = E.signal(ins) if sig else E.pending()
        self._post(tok, reads, writes)
        return tok

    def dma(self, q, out, in_, reads=(), writes=(), slot=None):
        E = self.engs[q]
        self._pre(E, reads, writes)
        s = slot if slot is not None else (writes[0] if writes else reads[0])
        if s.dsem is None:
            self.nsem += 1
            s.dsem = self.es.enter_context(self.nc.semaphore("dm%d" % self.nsem))
        s.dcnt += 16
        E.e.dma_start(out=out, in_=in_).then_inc(s.dsem, 16)
        tok = (s.dsem, s.dcnt)
        self.dma_last[id(s.dsem)] = tok
        self._post(tok, reads, writes)
        return tok


def build_nc():
    nc = bass.Bass("TRN2", target_bir_lowering=False)
    x = nc.dram_tensor("x", [S, D], F32, kind="ExternalInput").ap()
    p_in = nc.dram_tensor("p", [S, 256], F32, kind="ExternalInput").ap()
    w_in = nc.dram_tensor("w_in", [D, DIN], F32, kind="ExternalInput").ap()
    w_out = nc.dram_tensor("w_out", [2048, D], F32, kind="ExternalInput").ap()
    w_ple = nc.dram_tensor("w_ple", [256, D], F32, kind="ExternalInput").ap()
    w_pg = nc.dram_tensor("w_pg", [D, D], F32, kind="ExternalInput").ap()
    w_r = nc.dram_tensor("w_r", [8, 128, 128], F32, kind="ExternalInput").ap()
    w_i = nc.dram_tensor("w_i", [8, 128, 128], F32, kind="ExternalInput").ap()
    cols_d = nc.dram_tensor("cols", [128, NCOLS], F32, kind="ExternalInput").ap()
    rows_d = nc.dram_tensor("rows", [3, D], F32, kind="ExternalInput").ap()
    bf_d = nc.dram_tensor("bf", [8, 1], F32, kind="ExternalInput").ap()
    out = nc.dram_tensor("out", [S, D], F32, kind="ExternalOutput").ap()
    ylT_d = nc.dram_tensor("ylT_d", [8, 128, S], BF16, kind="Internal").ap()
    caq_d = nc.dram_tensor("caq_d", [8, 3, S], BF16, kind="Internal").ap()
    cak_d = nc.dram_tensor("cak_d", [8, 3, S], BF16, kind="Internal").ap()

    with ExitStack() as es:
        c = Ctx(nc, es)
        banks = []
        bdeps = []
        for i in range(8):
            banks.append(es.enter_context(nc.psum_tensor("bank%d" % i, [128, 512], F32)))
            bdeps.append(Dep())

        def bank_bf(i):
            return banks[i][:].bitcast(BF16)

        xnT_d = [Dep() for _ in range(NT)]
        yaT = c.sb(es, "yaT", [128, 8, S], BF16)
        yaT_d = [[Dep() for _ in range(NCH)] for _ in range(8)]
        cols = c.sb(es, "cols", [128, NCOLS], F32); cols_dep = Dep()
        ncols = c.sb(es, "ncols", [128, NCOLS], F32); ncols_dep = Dep()
        cvec = c.sb(es, "cvec", [128, 16], F32); cvec_dep = Dep()
        onecol = c.sb(es, "onecol", [128, 1], F32); onecol_dep = Dep()
        epscol = c.sb(es, "epscol", [128, 1], F32); epscol_dep = Dep()
        identf = c.sb(es, "identf", [128, 128], F32); identf_dep = Dep()
        ident = c.sb(es, "ident", [128, 128], BF16); ident_dep = Dep()
        tri = c.sb(es, "tri", [128, 128], BF16); tri_dep = Dep()
        ones = c.sb(es, "ones", [128, 128], BF16); ones_dep = Dep()
        rstd_a = c.sb(es, "rstd_a", [128, NT], F32); rstd_a_dep = Dep()
        rstd_l = c.sb(es, "rstd_l", [128, NT], F32); rstd_l_dep = Dep()
        wst_dep = [Dep() for _ in range(2)]
        wst_i = [0]
        NST = [2]
        CAST_ENGS = ["pool"]
        ylT_dd = [[Dep() for _ in range(NCH)] for _ in range(8)]
        xs = ExitStack()
        wst = [c.sb(xs, "wst%d" % i, [128, 8, 128], F32) for i in range(2)]
        xbig = c.sb(xs, "xbig", [128, 8 * S], BF16)
        xnT = xbig[:].rearrange("p (k s) -> p k s", k=8)
        wo = xbig[:, 0:16 * D].rearrange("p (k n) -> p k n", k=16); wo_dep = Dep()
        wpg = xbig[:, 16 * D:24 * D].rearrange("p (k n) -> p k n", k=8); wpg_dep = Dep()
        wpl = xbig[:, 24 * D:26 * D].rearrange("p (k n) -> p k n", k=2); wpl_dep = Dep()
        rowsb = xbig[:, 26 * D:32 * D].bitcast(F32).rearrange("p (k n) -> p k n", k=3); rowsb_dep = Dep()

        def load_w(dst, dst_dep, src, kc, extra_writes=()):
            i = wst_i[0] % NST[0]
            eng = CAST_ENGS[wst_i[0] % len(CAST_ENGS)]
            wst_i[0] += 1
            ncl = src.shape[1]
            st = wst[i][:, 0:kc, 0:ncl]
            c.dma("sp", st, src.rearrange("(k p) n -> p k n", p=128), writes=[wst_dep[i]])
            if eng == "act":
                c.op("act", lambda e: e.activation(out=dst, in_=st, func=AF.Copy), reads=[wst_dep[i]],
                     writes=[dst_dep] + list(extra_writes))
            else:
                c.op(eng, lambda e: e.tensor_copy(out=dst, in_=st), reads=[wst_dep[i]],
                     writes=[dst_dep] + list(extra_writes))

        def load_w_split(dst, dst_dep, src, kc, extra_writes=(), eng="dve"):
            st_ = {}

            def dma_fn():
                i = wst_i[0] % NST[0]
                wst_i[0] += 1
                st_["i"] = i
                st_["st"] = wst[i][:, 0:kc, 0:src.shape[1]]
                c.dma("sp", st_["st"], src.rearrange("(k p) n -> p k n", p=128), writes=[wst_dep[i]])

            def cast_fn():
                i = st_["i"]
                c.op(eng, lambda e: e.tensor_copy(out=dst, in_=st_["st"]), reads=[wst_dep[i]],
                     writes=[dst_dep] + list(extra_writes))
            return dma_fn, cast_fn

        def rsqrt_mean(e_dst, src, dep_dst, dep_src, tmp, tmp_dep, n=1024.0):
            c.op("act", lambda e: e.activation(out=tmp, in_=src, func=AF.Ln, scale=1.0 / n, bias=epscol[:, 0:1]),
                 reads=[dep_src, epscol_dep], writes=[tmp_dep])
            c.op("act", lambda e: e.activation(out=e_dst, in_=tmp, func=AF.Exp, scale=-0.5),
                 reads=[tmp_dep], writes=[dep_dst])

        c.dma("sp", cols[:], cols_d[:, :], writes=[cols_dep])
        c.op("pool", lambda e: e.memset(onecol[:], 1.0), writes=[onecol_dep])
        c.op("pool", lambda e: e.memset(epscol[:], EPS), writes=[epscol_dep])
        c.op("pool", lambda e: e.memset(ones[:], 1.0), writes=[ones_dep])
        c.op("pool", lambda e: e.memset(identf[:], 1.0), writes=[identf_dep])
        c.op("pool", lambda e: e.affine_select(out=identf[:], in_=identf[:], pattern=[[1, 128]],
                                                compare_op=ALU.is_equal, fill=0.0, base=0,
                                                channel_multiplier=-1), reads=[identf_dep], writes=[identf_dep])
        c.op("dve", lambda e: e.tensor_copy(out=ident[:], in_=identf[:]), reads=[identf_dep], writes=[ident_dep])
        c.op("pool", lambda e: e.memset(identf[:], 1.0), reads=[], writes=[identf_dep])
        c.op("pool", lambda e: e.affine_select(out=identf[:], in_=identf[:], pattern=[[1, 128]],
                                                compare_op=ALU.is_ge, fill=0.0, base=0,
                                                channel_multiplier=-1), reads=[identf_dep], writes=[identf_dep])
        c.op("dve", lambda e: e.tensor_copy(out=tri[:], in_=identf[:]), reads=[identf_dep], writes=[tri_dep])
        c.op("dve", lambda e: e.tensor_scalar(out=ncols[:], in0=cols[:], scalar1=-1.0, scalar2=None,
                                               op0=ALU.mult), reads=[cols_dep], writes=[ncols_dep])
        c.op("act", lambda e: e.activation(out=cvec[:, 0:8], in_=cols[:, C_LAM:C_LAM + 8], func=AF.Exp, scale=-1.0),
             reads=[cols_dep], writes=[cvec_dep])
        c.op("act", lambda e: e.activation(out=cvec[:, 0:8], in_=cvec[:, 0:8], func=AF.Ln, bias=onecol[:, 0:1]),
             reads=[cvec_dep, onecol_dep], writes=[cvec_dep])
        c.op("dve", lambda e: e.tensor_scalar(out=cvec[:, 8:16], in0=cvec[:, 0:8], scalar1=-16.0, scalar2=None,
                                               op0=ALU.mult), reads=[cvec_dep], writes=[cvec_dep])
        c.op("dve", lambda e: e.tensor_scalar(out=cvec[:, 0:8], in0=cvec[:, 0:8], scalar1=-8.0, scalar2=None,
                                               op0=ALU.mult), reads=[cvec_dep], writes=[cvec_dep])

        with ExitStack() as ph:
            xt = [c.sb(ph, "xt%d" % i, [128, D], F32) for i in range(4)]
            xt_dep = [Dep() for _ in range(4)]
            xjunk = c.sb(ph, "xjunk", [128, D], BF16); xjunk_dep = Dep()
            xnb = [c.sb(ph, "xnb%d" % i, [128, D], BF16) for i in range(3)]
            xnb_dep = [Dep() for _ in range(3)]
            ssq0 = c.sb(ph, "ssq0", [128, NT], F32); ssq0_dep = [Dep() for _ in range(NT)]
            rs0 = c.sb(ph, "rs0", [128, NT], F32); rs0_dep = [Dep() for _ in range(NT)]
            tmp0 = c.sb(ph, "tmp0", [128, NT], F32); tmp0_dep = [Dep() for _ in range(NT)]
            def p0_load(tt):
                b = tt % 4
                c.dma(("sp", "pool")[tt % 2], xt[b][:], x[tt * 128:(tt + 1) * 128, :], writes=[xt_dep[b]])

            def p0_a(tt):
                b = tt % 4
                b2 = tt % 3
                c.op("act", lambda e: e.activation(out=xjunk[:], in_=xt[b][:], func=AF.Square,
                                                    accum_out=ssq0[:, tt:tt + 1]),
                     reads=[xt_dep[b]], writes=[xjunk_dep, ssq0_dep[tt]])
                rsqrt_mean(rs0[:, tt:tt + 1], ssq0[:, tt:tt + 1], rs0_dep[tt], ssq0_dep[tt],
                           tmp0[:, tt:tt + 1], tmp0_dep[tt])
                if tt % 4 != 3:
                    c.op("dve", lambda e: e.tensor_scalar(out=xnb[b2][:], in0=xt[b][:], scalar1=rs0[:, tt:tt + 1],
                                                           scalar2=None, op0=ALU.mult),
                         reads=[xt_dep[b], rs0_dep[tt]], writes=[xnb_dep[b2]])
                else:
                    c.op("act", lambda e: e.activation(out=xnb[b2][:], in_=xt[b][:], func=AF.Copy,
                                                        scale=rs0[:, tt:tt + 1]),
                         reads=[xt_dep[b], rs0_dep[tt]], writes=[xnb_dep[b2]])

            def p0_b(tt):
                b2 = tt % 3
                pb = tt % 2
                pv = bank_bf(pb)
                for kc in range(8):
                    c.op("pe", lambda e: e.transpose(out=pv[:, kc * 128:(kc + 1) * 128],
                                                     in_=xnb[b2][:, kc * 128:(kc + 1) * 128], identity=ident[:]),
                         reads=[xnb_dep[b2], ident_dep], writes=[bdeps[pb]], sig=(kc == 7))
                c.op("dve", lambda e: e.tensor_tensor(
                    out=xnT[:, :, tt * 128:(tt + 1) * 128],
                    in0=pv.rearrange("p (k t) -> p k t", k=8),
                    in1=cols[:, C_PRE:C_PRE + 8].unsqueeze(2).to_broadcast([128, 8, 128]),
                    op=ALU.mult), reads=[bdeps[pb], cols_dep], writes=[xnT_d[tt]])

            for tt in range(3):
                p0_load(tt)
            p0_a(0)
            p0_a(1)
            for tt in range(NT):
                if tt + 3 < NT:
                    p0_load(tt + 3)
                if tt + 2 < NT:
                    p0_a(tt + 2)
                p0_b(tt)

        c.barrier()

        def xn_reads(ch):
            return xnT_d[ch * 4:(ch + 1) * 4]

        def proj_chunk(wt, wdep, ch, pbank, M=128):
            for kc in range(8):
                c.op("pe", lambda e: e.matmul(banks[pbank][0:M, :], lhsT=wt[:, kc, 0:M],
                                              rhs=xnT[:, kc, ch * 512:(ch + 1) * 512],
                                              start=(kc == 0), stop=(kc == 7)),
                     reads=[wdep] + xn_reads(ch), writes=[bdeps[pbank]], sig=(kc == 7))

        def act_sigmoid(dst, dst_dep, src, src_deps, nbias=None, nbias_dep=None, P=128):
            if nbias is None:
                c.op("act", lambda e: e.activation(out=dst, in_=src, func=AF.Exp, scale=-1.0),
                     reads=src_deps, writes=[dst_dep])
            else:
                c.op("act", lambda e: e.activation(out=dst, in_=src, func=AF.Exp, scale=-1.0, bias=nbias),
                     reads=src_deps + [nbias_dep], writes=[dst_dep])
            c.op("act", lambda e: e.activation(out=dst, in_=dst, func=AF.Ln, bias=onecol[0:P, 0:1]),
                 reads=[dst_dep, onecol_dep], writes=[dst_dep])
            c.op("act", lambda e: e.activation(out=dst, in_=dst, func=AF.Exp, scale=-1.0),
                 reads=[dst_dep], writes=[dst_dep])

        with ExitStack() as ph:
            wx = [c.sb(ph, "wx%d" % i, [128, 8, 128], BF16) for i in range(2)]; wx_dep = [Dep(), Dep()]
            wg = [c.sb(ph, "wg%d" % i, [128, 8, 128], BF16) for i in range(2)]; wg_dep = [Dep(), Dep()]
            wr = [c.sb(ph, "wr%d" % i, [128, 1, 128], BF16) for i in range(2)]; wr_dep = [Dep(), Dep()]
            wi = [c.sb(ph, "wi%d" % i, [128, 1, 128], BF16) for i in range(2)]; wi_dep = [Dep(), Dep()]
            xl = c.sb(ph, "xl", [128, 3 + S], F32); xl_dep = [Dep() for _ in range(NCH)]; xl0_dep = Dep()

            def two(name, dt=F32):
                return ([c.sb(ph, "%s%d" % (name, i), [128, 512], dt) for i in range(2)], [Dep(), Dep()])
            xc, xc_dep = two("xc")
            xcb, xcb_dep = two("xcb", BF16)
            rr, rr_dep = two("rr")
            ii, ii_dep = two("ii")
            aa, aa_dep = two("aa")
            a2, a2_dep = two("a2")
            uu, uu_dep = two("uu")
            hh, hh_dep = two("hh")
            hsq, hsq_dep = two("hsq", BF16)
            eg, eg_dep = two("eg")
            yl, yl_dep = two("yl", BF16)
            ssql = c.sb(ph, "ssql", [128, NT], F32); ssql_dep = Dep()
            tmpl = c.sb(ph, "tmpl", [128, NT], F32); tmpl_dep = Dep()
            B_SSQ = 7
            rot = [0]

            def nb():
                b_ = rot[0] % 7
                rot[0] += 1
                return b_
            c.op("pool", lambda e: e.memset(xl[:, 0:3], 0.0), writes=[xl0_dep])
            pg_of = {}
            NG = 8 * NCH

            lw_jobs = []

            def lru_weight_jobs(n):
                wb = n % 2
                return [load_w_split(wx[wb][:], wx_dep[wb], w_in[:, OFF_XL + n * 128:OFF_XL + (n + 1) * 128], 8),
                        load_w_split(wg[wb][:], wg_dep[wb], w_in[:, OFF_GL + n * 128:OFF_GL + (n + 1) * 128], 8),
                        load_w_split(wr[wb][:], wr_dep[wb], w_r[n, :, :], 1),
                        load_w_split(wi[wb][:], wi_dep[wb], w_i[n, :, :], 1)]

            def lru_weights(n):
                wb = n % 2
                load_w(wx[wb][:], wx_dep[wb], w_in[:, OFF_XL + n * 128:OFF_XL + (n + 1) * 128], 8)
                load_w(wg[wb][:], wg_dep[wb], w_in[:, OFF_GL + n * 128:OFF_GL + (n + 1) * 128], 8)
                load_w(wr[wb][:], wr_dep[wb], w_r[n, :, :], 1)
                load_w(wi[wb][:], wi_dep[wb], w_i[n, :, :], 1)

            def S1p(g):
                n, ch = divmod(g, NCH)
                wb = n % 2
                b = g % 2
                if n + 1 < 8:
                    if ch == 2:
                        lw_jobs[:] = lru_weight_jobs(n + 1)
                        lw_jobs[0][0]()
                        lw_jobs[1][0]()
                    elif ch == 3:
                        lw_jobs[0][1]()
                        lw_jobs[1][1]()
                        lw_jobs[2][0]()
                        lw_jobs[3][0]()
                    elif ch == 4:
                        lw_jobs[2][1]()
                        lw_jobs[3][1]()
                pb = (0, 1)[g % 2]
                proj_chunk(wx[wb], wx_dep[wb], ch, pb)

            def S1e(g):
                n, ch = divmod(g, NCH)
                wb = n % 2
                b = g % 2
                pb = (0, 1)[g % 2]
                cs = slice(3 + ch * 512, 3 + (ch + 1) * 512)
                c.op("act", lambda e: e.activation(out=xl[:, cs], in_=banks[pb][:], func=AF.Copy),
                     reads=[bdeps[pb]], writes=[xl_dep[ch]])
                prev = [xl_dep[ch - 1]] if ch > 0 else [xl0_dep]
                base = ch * 512
                c.op("dve", lambda e: e.tensor_scalar(out=xc[b][:], in0=xl[:, base:base + 512],
                                                       scalar1=cols[:, C_CW + n:C_CW + n + 1],
                                                       scalar2=cols[:, C_CB + n:C_CB + n + 1],
                                                       op0=ALU.mult, op1=ALU.add),
                     reads=[xl_dep[ch], cols_dep] + prev, writes=[xc_dep[b]])
                for j in range(1, 4):
                    c.op("dve", lambda e: e.scalar_tensor_tensor(
                        out=xc[b][:], in0=xl[:, base + j:base + j + 512],
                        scalar=cols[:, C_CW + 8 * j + n:C_CW + 8 * j + n + 1], in1=xc[b][:],
                        op0=ALU.mult, op1=ALU.add),
                        reads=[xl_dep[ch], cols_dep, xc_dep[b]] + prev, writes=[xc_dep[b]])
                c.op("dve", lambda e: e.tensor_copy(out=xcb[b][:], in_=xc[b][:]), reads=[xc_dep[b]],
                     writes=[xcb_dep[b]])
                pg = (2, 3, 6)[g % 3]
                pg_of[g] = pg
                proj_chunk(wg[wb], wg_dep[wb], ch, pg)

            def S3(g):
                n, ch = divmod(g, NCH)
                wb = n % 2
                b = g % 2
                pr = 4
                pi_ = 5
                c.op("pe", lambda e: e.matmul(banks[pr][:], lhsT=wr[wb][:, 0, :], rhs=xcb[b][:],
                                              start=True, stop=True),
                     reads=[wr_dep[wb], xcb_dep[b]], writes=[bdeps[pr]])
                c.op("pe", lambda e: e.matmul(banks[pi_][:], lhsT=wi[wb][:, 0, :], rhs=xcb[b][:],
                                              start=True, stop=True),
                     reads=[wi_dep[wb], xcb_dep[b]], writes=[bdeps[pi_]])
                c.op("act", lambda e: e.activation(out=rr[b][:], in_=banks[pr][:], func=AF.Exp, scale=-1.0,
                                                    bias=ncols[:, C_BR + n:C_BR + n + 1]),
                     reads=[bdeps[pr], ncols_dep], writes=[rr_dep[b]])
                if g + 1 < NG:
                    S1e(g + 1)
                c.op("act", lambda e: e.activation(out=rr[b][:], in_=rr[b][:], func=AF.Ln, bias=onecol[:, 0:1]),
                     reads=[rr_dep[b], onecol_dep], writes=[rr_dep[b]])
                c.op("act", lambda e: e.activation(out=rr[b][:], in_=rr[b][:], func=AF.Exp, scale=-1.0),
                     reads=[rr_dep[b]], writes=[rr_dep[b]])
                c.op("act", lambda e: e.activation(out=aa[b][:], in_=rr[b][:], func=AF.Exp, scale=cvec[:, n:n + 1]),
                     reads=[rr_dep[b], cvec_dep], writes=[aa_dep[b]])
                c.op("pool", lambda e: e.tensor_tensor(out=a2[b][:], in0=aa[b][:], in1=aa[b][:], op=ALU.mult),
                     reads=[aa_dep[b]], writes=[a2_dep[b]])
                act_sigmoid(ii[b][:], ii_dep[b], banks[pi_][:], [bdeps[pi_]],
                            ncols[:, C_BI + n:C_BI + n + 1], ncols_dep)
                c.op("act", lambda e: e.activation(out=a2[b][:], in_=a2[b][:], func=AF.Ln, scale=-1.0,
                                                    bias=onecol[:, 0:1]),
                     reads=[a2_dep[b], onecol_dep], writes=[a2_dep[b]])
                c.op("act", lambda e: e.activation(out=a2[b][:], in_=a2[b][:], func=AF.Exp, scale=0.5),
                     reads=[a2_dep[b]], writes=[a2_dep[b]])

            def S3b(g):
                n, ch = divmod(g, NCH)
                wb = n % 2
                b = g % 2
                c.op("pool", lambda e: e.tensor_tensor(out=uu[b][:], in0=ii[b][:], in1=xc[b][:], op=ALU.mult),
                     reads=[ii_dep[b], xc_dep[b]], writes=[uu_dep[b]])
                c.op("pool", lambda e: e.tensor_tensor(out=uu[b][:], in0=uu[b][:], in1=a2[b][:], op=ALU.mult),
                     reads=[uu_dep[b], a2_dep[b]], writes=[uu_dep[b]])
                if ch == 0:
                    c.op("dve", lambda e: e.tensor_tensor_scan(out=hh[b][:], data0=aa[b][:], data1=uu[b][:],
                                                                initial=0.0, op0=ALU.mult, op1=ALU.add),
                         reads=[aa_dep[b], uu_dep[b]], writes=[hh_dep[b]])
                else:
                    c.op("dve", lambda e: e.tensor_tensor_scan(out=hh[b][:], data0=aa[b][:], data1=uu[b][:],
                                                                initial=hh[1 - b][:, 511:512],
                                                                op0=ALU.mult, op1=ALU.add),
                         reads=[aa_dep[b], uu_dep[b], hh_dep[1 - b]], writes=[hh_dep[b]])
                c.op("pool", lambda e: e.tensor_tensor(out=hsq[b][:], in0=hh[b][:], in1=hh[b][:], op=ALU.mult),
                     reads=[hh_dep[b]], writes=[hsq_dep[b]])
                pg = pg_of.pop(g)
                act_sigmoid(eg[b][:], eg_dep[b], banks[pg][:], [bdeps[pg]])
                c.op("dve", lambda e: e.tensor_tensor(out=eg[b][:], in0=banks[pg][:], in1=eg[b][:], op=ALU.mult),
                     reads=[bdeps[pg], eg_dep[b]], writes=[eg_dep[b]])
                c.op("dve", lambda e: e.scalar_tensor_tensor(out=yl[b][:], in0=hh[b][:],
                                                              scalar=cols[:, C_LG + n:C_LG + n + 1],
                                                              in1=eg[b][:], op0=ALU.mult, op1=ALU.mult),
                     reads=[hh_dep[b], eg_dep[b], cols_dep], writes=[yl_dep[b]])
                c.dma("sp", ylT_d[n, :, ch * 512:(ch + 1) * 512], yl[b][:], reads=[yl_dep[b]],
                      writes=[ylT_dd[n][ch]], slot=yl_dep[b])

            def SSQ(g):
                n, ch = divmod(g, NCH)
                b = g % 2
                for q in range(4):
                    tt = ch * 4 + q
                    col = n * NT + tt
                    c.op("pe", lambda e: e.matmul(banks[B_SSQ][:, col:col + 1],
                                                  lhsT=hsq[b][:, q * 128:(q + 1) * 128], rhs=ones[:, 0:1],
                                                  start=True, stop=True),
                         reads=[hsq_dep[b], ones_dep], writes=[bdeps[B_SSQ]], sig=(q == 3))

            lru_weights(0)
            S1p(0)
            S1p(1)
            S1e(0)
            for g in range(NG):
                S3(g)
                if g + 2 < NG:
                    S1p(g + 2)
                S3b(g)
                if g >= 1:
                    SSQ(g - 1)
            SSQ(NG - 1)
            c.op("dve", lambda e: e.tensor_reduce(out=ssql[:], in_=banks[B_SSQ][:, 0:256].rearrange(
                "p (n t) -> p t n", n=8), axis=AX.X, op=ALU.add), reads=[bdeps[B_SSQ]], writes=[ssql_dep])
            rsqrt_mean(rstd_l[:], ssql[:], rstd_l_dep, ssql_dep, tmpl[:], tmpl_dep)

        c.barrier()
        with ExitStack() as ph:
            B_SC = ((0, 1), (2, 3))
            B_NUM, B_DEN, B_MISC, B_SSQ = 4, 5, 6, 7
            caq_dd = [Dep() for _ in range(NCH)]
            cak_dd = [Dep() for _ in range(NCH)]
            with ExitStack() as ph2:
                wf = c.sb(ph2, "wf", [128, 8, 8], BF16); wf_dep = Dep()
                nbf = c.sb(ph2, "nbf", [8, 1], F32); nbf_dep = Dep()
                one8 = c.sb(ph2, "one8", [8, 512], F32); one8_dep = Dep()
                lsp = [c.sb(ph2, "lsp%d" % i, [8, 512], F32) for i in range(2)]; lsp_dep = [Dep(), Dep()]
                ncs = [c.sb(ph2, "ncs%d" % i, [8, 512], F32) for i in range(2)]; ncs_dep = [Dep(), Dep()]
                res = c.sb(ph2, "res", [8, 512], F32); res_dep = Dep()
                cpos = [c.sb(ph2, "cpos%d" % i, [8, 3, 512], BF16) for i in range(2)]; cpos_dep = [Dep(), Dep()]
                cneg = [c.sb(ph2, "cneg%d" % i, [8, 3, 512], BF16) for i in range(2)]; cneg_dep = [Dep(), Dep()]
                pass
                load_w(wf[:], wf_dep, w_in[:, OFF_F:OFF_F + 8], 8)
                c.dma("sp", nbf[:], bf_d[:, :], writes=[nbf_dep])
                c.op("dve", lambda e: e.tensor_scalar(out=nbf[:], in0=nbf[:], scalar1=-1.0, scalar2=None,
                                                       op0=ALU.mult), reads=[nbf_dep], writes=[nbf_dep])
                c.op("pool", lambda e: e.memset(one8[:], 1.0), writes=[one8_dep])
                for ch in range(NCH):
                    b = ch % 2
                    proj_chunk(wf, wf_dep, ch, B_MISC, M=8)
                    c.op("act", lambda e: e.activation(out=lsp[b][:], in_=banks[B_MISC][0:8, :], func=AF.Exp,
                                                        scale=-1.0, bias=nbf[:, 0:1]),
                         reads=[bdeps[B_MISC], nbf_dep], writes=[lsp_dep[b]])
                    c.op("act", lambda e: e.activation(out=lsp[b][:], in_=lsp[b][:], func=AF.Ln,
                                                        bias=onecol[0:8, 0:1]),
                         reads=[lsp_dep[b], onecol_dep], writes=[lsp_dep[b]])
                    if ch == 0:
                        c.op("dve", lambda e: e.tensor_tensor_scan(out=ncs[b][:], data0=one8[:], data1=lsp[b][:],
                                                                    initial=0.0, op0=ALU.mult, op1=ALU.add),
                             reads=[one8_dep, lsp_dep[b]], writes=[ncs_dep[b]])
                    else:
                        c.op("dve", lambda e: e.tensor_tensor_scan(out=ncs[b][:], data0=one8[:], data1=lsp[b][:],
                                                                    initial=ncs[1 - b][:, 511:512],
                                                                    op0=ALU.mult, op1=ALU.add),
                             reads=[one8_dep, lsp_dep[b], ncs_dep[1 - b]], writes=[ncs_dep[b]])
                    c.op("dve", lambda e: e.tensor_copy(out=cpos[b][:, 0, :], in_=ncs[b][:]),
                         reads=[ncs_dep[b]], writes=[cpos_dep[b]])
                    c.op("dve", lambda e: e.tensor_tensor(out=res[:], in0=ncs[b][:], in1=cpos[b][:, 0, :],
                                                           op=ALU.subtract),
                         reads=[ncs_dep[b], cpos_dep[b]], writes=[res_dep])
                    c.op("dve", lambda e: e.tensor_copy(out=cpos[b][:, 1, :], in_=res[:]),
                         reads=[res_dep], writes=[cpos_dep[b]])
                    c.op("dve", lambda e: e.tensor_tensor(out=res[:], in0=res[:], in1=cpos[b][:, 1, :],
                                                           op=ALU.subtract),
                         reads=[res_dep, cpos_dep[b]], writes=[res_dep])
                    c.op("dve", lambda e: e.tensor_copy(out=cpos[b][:, 2, :], in_=res[:]),
                         reads=[res_dep], writes=[cpos_dep[b]])
                    c.op("dve", lambda e: e.tensor_scalar(out=cneg[b][:], in0=cpos[b][:], scalar1=-1.0,
                                                           scalar2=None, op0=ALU.mult),
                         reads=[cpos_dep[b]], writes=[cneg_dep[b]])
                    c.dma("sp", caq_d[:, :, ch * 512:(ch + 1) * 512], cneg[b][:], reads=[cneg_dep[b]],
                          writes=[caq_dd[ch]], slot=cneg_dep[b])
                    c.dma("sp", cak_d[:, :, ch * 512:(ch + 1) * 512], cpos[b][:], reads=[cpos_dep[b]],
                          writes=[cak_dd[ch]], slot=cpos_dep[b])

            c.barrier()
            wq = c.sb(ph, "wq", [128, 8, 128], BF16); wq_dep = Dep()
            wk = c.sb(ph, "wk", [128, 8, 128], BF16); wk_dep = Dep()
            wv = c.sb(ph, "wv", [128, 8, 128], BF16); wv_dep = Dep()
            wga = c.sb(ph, "wga", [128, 8, 128], BF16); wga_dep = Dep()
            sgT = c.sb(ph, "sgT", [128, S], BF16); sgT_dep = [Dep() for _ in range(NCH)]
            qT = c.sb(ph, "qT", [128, S], BF16); qT_dep = [Dep() for _ in range(NCH)]
            kT = c.sb(ph, "kT", [128, S], BF16); kT_dep = [Dep() for _ in range(NCH)]
            vv = c.sb(ph, "vv", [128, NT, 128], BF16); vv_dep = [Dep() for _ in range(NCH)]
            augk = c.sb(ph, "augk", [128, S], BF16); augk_dep = Dep()
            augq = [c.sb(ph, "augq%d" % i, [128, 512], BF16) for i in range(2)]; augq_dep = [Dep(), Dep()]
            NE = 6
            EE = [c.sb(ph, "EE%d" % i, [128, 512], BF16) for i in range(NE)]; EE_dep = [Dep() for _ in range(NE)]
            rden = c.sb(ph, "rden", [128, 512], F32); rden_dep = Dep()
            oo = [c.sb(ph, "oo%d" % i, [128, 512], F32) for i in range(2)]; oo_dep = [Dep(), Dep()]
            osq = [c.sb(ph, "osq%d" % i, [128, 512], BF16) for i in range(2)]; osq_dep = [Dep(), Dep()]
            ega = c.sb(ph, "ega", [128, 512], F32); ega_dep = Dep()
            ssqa = c.sb(ph, "ssqa", [128, NT], F32); ssqa_dep = Dep()
            tmpa = c.sb(ph, "tmpa", [128, NT], F32); tmpa_dep = Dep()
            c.op("pool", lambda e: e.memset(augk[:], 0.0), writes=[augk_dep])
            c.op("pool", lambda e: e.memset(augk[0:6, :], 1.0), writes=[augk_dep])
            for i in range(2):
                c.op("pool", lambda e: e.memset(augq[i][:], 0.0), writes=[augq_dep[i]])
                c.op("pool", lambda e: e.memset(augq[i][0:6, :], 1.0), writes=[augq_dep[i]])
            B_SCS = (0, 1, 2, 6)
            B_SSQ = 3
            B_ND = ((4, 5), (7, 5))
            cnt = {"e": 0, "s": 0, "q": 0, "p": 0, "c": 0}
            PROJ_BANKS = (0, 1, 2)

            def head_weight_jobs(h):
                return [load_w_split(wq[:], wq_dep, w_in[:, OFF_Q + h * 128:OFF_Q + (h + 1) * 128], 8),
                        load_w_split(wk[:], wk_dep, w_in[:, OFF_K + h * 128:OFF_K + (h + 1) * 128], 8),
                        load_w_split(wv[:], wv_dep, w_in[:, OFF_V + h * 128:OFF_V + (h + 1) * 128], 8),
                        load_w_split(wga[:], wga_dep, w_in[:, OFF_GA + h * 128:OFF_GA + (h + 1) * 128], 8)]

            def head_inproj(h):
                if h == 0:
                    for dfn, cfn in head_weight_jobs(0):
                        dfn()
                        cfn()
                c.dma("sp", augk[3:6, :], cak_d[h, :, :], reads=cak_dd, writes=[augk_dep])
                for ch in range(NCH):
                    pb = PROJ_BANKS[cnt["p"] % 3]; cnt["p"] += 1
                    proj_chunk(wq, wq_dep, ch, pb)
                    c.op("act", lambda e: e.activation(out=qT[:, ch * 512:(ch + 1) * 512], in_=banks[pb][:],
                                                        func=AF.Copy, scale=SCALE),
                         reads=[bdeps[pb]], writes=[qT_dep[ch]])
                    pb = PROJ_BANKS[cnt["p"] % 3]; cnt["p"] += 1
                    proj_chunk(wk, wk_dep, ch, pb)
                    c.op("dve", lambda e: e.tensor_copy(out=kT[:, ch * 512:(ch + 1) * 512], in_=banks[pb][:]),
                         reads=[bdeps[pb]], writes=[kT_dep[ch]])
                    while carry:
                        carry.pop(0)()
                    pb = PROJ_BANKS[cnt["p"] % 3]; cnt["p"] += 1
                    for q in range(4):
                        tt = ch * 4 + q
                        for kc in range(8):
                            c.op("pe", lambda e: e.matmul(banks[pb][:, q * 128:(q + 1) * 128],
                                                          lhsT=xnT[:, kc, tt * 128:(tt + 1) * 128],
                                                          rhs=wv[:, kc, :], start=(kc == 0), stop=(kc == 7)),
                                 reads=[wv_dep, xnT_d[tt]], writes=[bdeps[pb]], sig=(kc == 7 and q == 3))
                    c.op("act", lambda e: e.activation(
                        out=vv[:, ch * 4:(ch + 1) * 4, :],
                        in_=banks[pb][:].rearrange("p (q d) -> p q d", q=4), func=AF.Copy),
                        reads=[bdeps[pb]], writes=[vv_dep[ch]])
                    pb = PROJ_BANKS[cnt["p"] % 3]; cnt["p"] += 1
                    proj_chunk(wga, wga_dep, ch, pb)
                    act_sigmoid(ega[:], ega_dep, banks[pb][:], [bdeps[pb]])
                    c.op("dve", lambda e: e.scalar_tensor_tensor(out=sgT[:, ch * 512:(ch + 1) * 512],
                                                                  in0=banks[pb][:],
                                                                  scalar=cols[:, C_AG + h:C_AG + h + 1],
                                                                  in1=ega[:], op0=ALU.mult, op1=ALU.mult),
                         reads=[bdeps[pb], ega_dep, cols_dep], writes=[sgT_dep[ch]])

            ES = [c.sb(ph, "ES%d" % i, [128, 512], BF16) for i in range(2)]; ES_dep = [Dep(), Dep()]

            def make_block(h, I, j, n0, ncol, diag, first, last, aq, nd, gpos, glen, gfirst, glast, es, gst):
                st = {}
                bnum, bden = B_ND[nd]

                def qk():
                    bk = B_SCS[cnt["s"] % len(B_SCS)]; cnt["s"] += 1
                    st["bk"] = bk
                    c.op("pe", lambda e: e.matmul(banks[bk][:, 0:ncol], lhsT=kT[:, j * 128:(j + 1) * 128],
                                                  rhs=qT[:, I * 512 + n0:(I + 1) * 512],
                                                  start=True, stop=False),
                         reads=[kT_dep[j // 4], qT_dep[I]], writes=[bdeps[bk]], sig=False)
                    c.op("pe", lambda e: e.matmul(banks[bk][:, 0:ncol], lhsT=augk[:, j * 128:(j + 1) * 128],
                                                  rhs=augq[aq][:, n0:512], start=False, stop=True),
                         reads=[augk_dep, augq_dep[aq]], writes=[bdeps[bk]], sig=True)

                def ex():
                    bk = st["bk"]
                    eb = cnt["e"] % NE; cnt["e"] += 1
                    st["eb"] = eb
                    c.op("act", lambda e: e.activation(out=EE[eb][:, 0:ncol], in_=banks[bk][:, 0:ncol],
                                                        func=AF.Exp),
                         reads=[bdeps[bk]], writes=[EE_dep[eb]])
                    if diag:
                        c.op("pool", lambda e: e.tensor_tensor(out=EE[eb][:, 0:128], in0=EE[eb][:, 0:128],
                                                                in1=tri[:], op=ALU.mult),
                             reads=[EE_dep[eb], tri_dep], writes=[EE_dep[eb]])
                    if glen > 1:
                        if gpos == 0:
                            gst["eb0"] = eb
                            gst["n00"] = n0
                        elif gpos == 1:
                            eb0 = gst["eb0"]
                            d0 = n0 - gst["n00"]
                            if d0 > 0:
                                c.op("pool", lambda e: e.tensor_copy(out=ES[es][:, gst["n00"]:n0],
                                                                      in_=EE[eb0][:, 0:d0]),
                                     reads=[EE_dep[eb0]], writes=[ES_dep[es]])
                            c.op("dve", lambda e: e.tensor_tensor(out=ES[es][:, n0:n0 + ncol],
                                                                   in0=EE[eb0][:, d0:d0 + ncol],
                                                                   in1=EE[eb][:, 0:ncol], op=ALU.add),
                                 reads=[EE_dep[eb0], EE_dep[eb]], writes=[ES_dep[es]])
                        else:
                            c.op("dve", lambda e: e.tensor_tensor(out=ES[es][:, n0:n0 + ncol],
                                                                   in0=ES[es][:, n0:n0 + ncol],
                                                                   in1=EE[eb][:, 0:ncol], op=ALU.add),
                                 reads=[ES_dep[es], EE_dep[eb]], writes=[ES_dep[es]])

                def pv():
                    eb = st["eb"]
                    has_den = (glen == 1) or (gpos == glen - 1)
                    c.op("pe", lambda e: e.matmul(banks[bnum][:, n0:n0 + ncol], lhsT=vv[:, j, :],
                                                  rhs=EE[eb][:, 0:ncol], start=first, stop=last),
                         reads=[vv_dep[j // 4], EE_dep[eb]], writes=[bdeps[bnum]], sig=(not has_den))
                    if glen == 1:
                        c.op("pe", lambda e: e.matmul(banks[bden][:, n0:n0 + ncol], lhsT=ones[:],
                                                      rhs=EE[eb][:, 0:ncol], start=gfirst, stop=glast),
                             reads=[ones_dep, EE_dep[eb]], writes=[bdeps[bden]], sig=True)
                    elif gpos == glen - 1:
                        g0 = gst["n00"]
                        c.op("pe", lambda e: e.matmul(banks[bden][:, g0:512], lhsT=ones[:],
                                                      rhs=ES[es][:, g0:512], start=gfirst, stop=glast),
                             reads=[ones_dep, ES_dep[es]], writes=[bdeps[bden]], sig=True)
                return qk, ex, pv

            def make_pre(h, I, aq):
                def pre():
                    c.dma("sp", augq[aq][0:3, :], caq_d[h, :, I * 512:(I + 1) * 512], reads=[caq_dd[I]],
                          writes=[augq_dep[aq]])
                return pre

            def make_final(h, I, nd):
                bnum, bden = B_ND[nd]
                ob = I % 2
                qs = slice(I * 512, (I + 1) * 512)

                def fin():
                    c.op("act", lambda e: e.activation(out=rden[:], in_=banks[bden][:], func=AF.Ln),
                         reads=[bdeps[bden]], writes=[rden_dep])
                    c.op("act", lambda e: e.activation(out=rden[:], in_=rden[:], func=AF.Exp, scale=-1.0),
                         reads=[rden_dep], writes=[rden_dep])
                    c.op("dve", lambda e: e.tensor_tensor(out=oo[ob][:], in0=banks[bnum][:], in1=rden[:],
                                                           op=ALU.mult),
                         reads=[bdeps[bnum], rden_dep], writes=[oo_dep[ob]])
                    c.op("dve", lambda e: e.tensor_tensor(out=osq[ob][:], in0=oo[ob][:], in1=oo[ob][:],
                                                           op=ALU.mult),
                         reads=[oo_dep[ob]], writes=[osq_dep[ob]])
                    c.op("dve", lambda e: e.tensor_tensor(out=yaT[:, h, qs], in0=oo[ob][:], in1=sgT[:, qs],
                                                           op=ALU.mult),
                         reads=[oo_dep[ob], sgT_dep[I]], writes=[yaT_d[h][I]])

                def fin_pe():
                    for q in range(4):
                        col = h * NT + I * 4 + q
                        c.op("pe", lambda e: e.matmul(banks[B_SSQ][:, col:col + 1],
                                                      lhsT=osq[ob][:, q * 128:(q + 1) * 128], rhs=ones[:, 0:1],
                                                      start=True, stop=True),
                             reads=[osq_dep[ob], ones_dep], writes=[bdeps[B_SSQ]], sig=(q == 3))
                return fin, fin_pe

            pre_jobs = []
            for g_ in range(8):
                for kc2 in range(2):
                    pre_jobs.append(load_w_split(
                        wo[:, kc2 * 8:(kc2 + 1) * 8, g_ * 128:(g_ + 1) * 128], wo_dep,
                        w_out[kc2 * 1024:(kc2 + 1) * 1024, g_ * 128:(g_ + 1) * 128], 8, extra_writes=xnT_d))
                pre_jobs.append(load_w_split(wpg[:, :, g_ * 128:(g_ + 1) * 128], wpg_dep,
                                             w_pg[:, g_ * 128:(g_ + 1) * 128], 8, extra_writes=xnT_d))
                pre_jobs.append(load_w_split(wpl[:, :, g_ * 128:(g_ + 1) * 128], wpl_dep,
                                             w_ple[:, g_ * 128:(g_ + 1) * 128], 2, extra_writes=xnT_d))
            pend_casts = []
            carry = []
            for h in range(8):
                head_inproj(h)
                hw_jobs = head_weight_jobs(h + 1) if h < 7 else []
                tasks = []
                make_pre(h, 0, cnt["q"] % 2)()
                for I in range(NCH):
                    aq = cnt["q"] % 2; cnt["q"] += 1
                    nd = cnt["c"] % 2; cnt["c"] += 1
                    blks = [(j, 0, 512, False) for j in range(4 * I)]
                    blks += [(4 * I + r, 128 * r, 512 - 128 * r, True) for r in range(4)]
                    fin, fin_pe = make_final(h, I, nd)
                    ngrp = len(blks) // 4
                    gsts = [dict() for _ in range(ngrp)]
                    for bi, (j, n0, ncol, diag) in enumerate(blks):
                        gi_, gpos = divmod(bi, 4)
                        if gpos == 0:
                            cnt["g"] = cnt.get("g", 0) + 1
                        qk, ex, pv = make_block(h, I, j, n0, ncol, diag, bi == 0, bi == len(blks) - 1, aq, nd,
                                                gpos, 4, gi_ == 0, gi_ == ngrp - 1, cnt["g"] % 2, gsts[gi_])
                        tasks.append({"pre": make_pre(h, I + 1, 1 - aq) if (bi == 0 and I + 1 < NCH) else None,
                                      "qk": qk, "ex": ex,
                                      "pv": pv, "fin": fin if bi == len(blks) - 1 else None,
                                      "fin_pe": fin_pe if bi == len(blks) - 1 else None})
                deferred = []

                def front(t):
                    if t["pre"] is not None:
                        t["pre"]()
                    t["qk"]()
                    t["ex"]()
                SK = 4
                for k_ in range(SK):
                    front(tasks[k_])
                for ti, t in enumerate(tasks):
                    if ti + SK < len(tasks):
                        front(tasks[ti + SK])
                    t["pv"]()
                    if h < 7:
                        if ti in (20, 28, 36, 44):
                            dfn, cfn = hw_jobs.pop(0)
                            dfn()
                            pend_casts.append((ti + 6, cfn))
                        while pend_casts and pend_casts[0][0] <= ti:
                            pend_casts.pop(0)[1]()
                    if h == 7:
                        if ti % 4 == 3 and pre_jobs:
                            dfn, cfn = pre_jobs.pop(0)
                            dfn()
                            pend_casts.append((ti + 6, cfn))
                        while pend_casts and pend_casts[0][0] <= ti:
                            pend_casts.pop(0)[1]()
                    due = [d for d in deferred if d[0] <= ti]
                    deferred = [d for d in deferred if d[0] > ti]
                    for d in due:
                        d[1]()
                    if t["fin"] is not None:
                        t["fin"]()
                        deferred.append((ti + 7, t["fin_pe"]))
                carry[:] = [d[1] for d in deferred]
            while carry:
                carry.pop(0)()
            while pend_casts:
                pend_casts.pop(0)[1]()
            while pre_jobs:
                dfn, cfn = pre_jobs.pop(0)
                dfn()
                cfn()
            for i_ in range(3):
                c.dma("sp", rowsb[:, i_, :], rows_d[i_:i_ + 1, :].partition_broadcast(128),
                      writes=[rowsb_dep] + xnT_d)
            c.op("dve", lambda e: e.tensor_reduce(out=ssqa[:], in_=banks[B_SSQ][:, 0:256].rearrange(
                "p (n t) -> p t n", n=8), axis=AX.X, op=ALU.add), reads=[bdeps[B_SSQ]], writes=[ssqa_dep])
            rsqrt_mean(rstd_a[:], ssqa[:], rstd_a_dep, ssqa_dep, tmpa[:], tmpa_dep)

        c.barrier()
        out_toks = []
        with ExitStack() as ph:
            ylc = [c.sb(ph, "ylc%d" % i, [128, 8, 512], BF16) for i in range(2)]; ylc_dep = [Dep(), Dep()]
            xt = [c.sb(ph, "xt3_%d" % i, [128, D], F32) for i in range(2)]; xt_dep = [Dep(), Dep()]
            pt = [c.sb(ph, "pt%d" % i, [128, 256], F32) for i in range(2)]; pt_dep = [Dep(), Dep()]
            ptb = c.sb(ph, "ptb", [128, 256], BF16); ptb_dep = Dep()
            pT = [c.sb(ph, "pT%d" % i, [128, 2, 128], BF16) for i in range(2)]; pT_dep = [Dep(), Dep()]
            t1 = c.sb(ph, "t1", [128, D], F32); t1_dep = Dep()
            mix = c.sb(ph, "mix", [128, D], F32); mix_dep = Dep()
            h1 = [c.sb(ph, "h1_%d" % i, [128, D], F32) for i in range(2)]; h1_dep = [Dep(), Dep()]
            h1b = c.sb(ph, "h1b", [128, D], BF16); h1b_dep = Dep()
            h1T = [c.sb(ph, "h1T%d" % i, [128, 8, 128], BF16) for i in range(2)]; h1T_dep = [Dep(), Dep()]
            gz = c.sb(ph, "gz", [128, D], F32); gz_dep = Dep()
            ee1 = c.sb(ph, "ee", [128, D], F32); ee = [ee1, ee1]; ee_d1 = Dep(); ee_dep = [ee_d1, ee_d1]
            ot = [c.sb(ph, "ot%d" % i, [128, D], F32) for i in range(2)]; ot_dep = [Dep(), Dep()]
            sm = [c.sb(ph, "sm%d" % i, [128, 8], F32) for i in range(2)]
            sm_dep = [[Dep() for _ in range(8)] for _ in range(2)]
            B_TR = 4
            B_GATE = (5, 6)
            B_E = (7, 4)

            def load_ylc(ch_):
                cb_ = ch_ % 2
                c.dma("sp", ylc[cb_][:], ylT_d[:, :, ch_ * 512:(ch_ + 1) * 512].rearrange("n p t -> p n t"),
                      reads=[ylT_dd[n][ch_] for n in range(8)], writes=[ylc_dep[cb_]])

            def AL(tt):
                b = tt % 2
                ch = tt // 4
                cb = ch % 2
                q = tt % 4
                if q == 0 and ch + 1 < NCH:
                    load_ylc(ch + 1)
                c.dma("sp", xt[b][:], x[tt * 128:(tt + 1) * 128, :], writes=[xt_dep[b]])
                c.dma("sp", pt[b][:], p_in[tt * 128:(tt + 1) * 128, :], writes=[pt_dep[b]])
                ts_ = slice(tt * 128, (tt + 1) * 128)
                for half in range(2):
                    for kc in range(8):
                        c.op("pe", lambda e: e.matmul(banks[half][:], lhsT=yaT[:, kc, ts_],
                                                      rhs=wo[:, kc, half * 512:(half + 1) * 512],
                                                      start=(kc == 0), stop=(kc == 7)),
                             reads=[yaT_d[kc][ch], wo_dep], writes=[bdeps[half]], sig=(kc == 7))
                for half in range(2):
                    for kc in range(8):
                        c.op("pe", lambda e: e.matmul(banks[2 + half][:], lhsT=ylc[cb][:, kc, q * 128:(q + 1) * 128],
                                                      rhs=wo[:, 8 + kc, half * 512:(half + 1) * 512],
                                                      start=(kc == 0), stop=(kc == 7)),
                             reads=[ylc_dep[cb], wo_dep], writes=[bdeps[2 + half]], sig=(kc == 7))

            def rsq_ops(dst, src, dep_dst, dep_src, tmp, tmp_dep, n=1024.0):
                return [
                    lambda: c.op("act", lambda e: e.activation(out=tmp, in_=src, func=AF.Ln, scale=1.0 / n,
                                                                bias=epscol[:, 0:1]),
                                 reads=[dep_src, epscol_dep], writes=[tmp_dep]),
                    lambda: c.op("act", lambda e: e.activation(out=dst, in_=tmp, func=AF.Exp, scale=-0.5),
                                 reads=[tmp_dep], writes=[dep_dst]),
                ]

            def chain_head(tt):
                b = tt % 2
                c.op("pool", lambda e: e.tensor_copy(out=ptb[:], in_=pt[b][:]), reads=[pt_dep[b]], writes=[ptb_dep])
                for half in range(2):
                    hs = slice(half * 512, (half + 1) * 512)
                    c.op("act", lambda e: e.activation(out=t1[:, hs], in_=banks[half][:], func=AF.Copy,
                                                        scale=rstd_a[:, tt:tt + 1]),
                         reads=[bdeps[half], rstd_a_dep], writes=[t1_dep])
                    c.op("dve", lambda e: e.scalar_tensor_tensor(out=mix[:, hs], in0=banks[2 + half][:],
                                                                  scalar=rstd_l[:, tt:tt + 1], in1=t1[:, hs],
                                                                  op0=ALU.mult, op1=ALU.add),
                         reads=[bdeps[2 + half], rstd_l_dep, t1_dep], writes=[mix_dep])

            def chain_ops(tt):
                b = tt % 2
                ops = [lambda: c.op("act", lambda e: e.activation(out=h1b[:], in_=mix[:], func=AF.Square,
                                                                   accum_out=sm[b][:, 0:1]),
                                    reads=[mix_dep], writes=[h1b_dep, sm_dep[b][0]])]
                ops += rsq_ops(sm[b][:, 1:2], sm[b][:, 0:1], sm_dep[b][1], sm_dep[b][0], sm[b][:, 2:3], sm_dep[b][2])
                ops += [
                    lambda: c.op("dve", lambda e: e.scalar_tensor_tensor(out=h1[b][:], in0=mix[:],
                                                                          scalar=sm[b][:, 1:2], in1=rowsb[:, 0, :],
                                                                          op0=ALU.mult, op1=ALU.mult),
                                 reads=[mix_dep, sm_dep[b][1], rowsb_dep], writes=[h1_dep[b]]),
                    lambda: c.op("dve", lambda e: e.tensor_tensor(out=h1b[:], in0=h1[b][:], in1=xt[b][:],
                                                                   op=ALU.add),
                                 reads=[h1_dep[b], xt_dep[b]], writes=[h1b_dep]),
                    lambda: c.op("dve", lambda e: e.tensor_tensor(out=h1[b][:], in0=h1[b][:], in1=xt[b][:],
                                                                   op=ALU.add),
                                 reads=[h1_dep[b], xt_dep[b]], writes=[h1_dep[b]]),
                ]
                return ops

            def TRp_ops(tt):
                b = tt % 2
                pv = bank_bf(B_TR)

                def tr_p():
                    for kc in range(2):
                        c.op("pe", lambda e: e.transpose(out=pv[:, kc * 128:(kc + 1) * 128],
                                                         in_=ptb[:, kc * 128:(kc + 1) * 128], identity=ident[:]),
                             reads=[ptb_dep, ident_dep], writes=[bdeps[B_TR]], sig=(kc == 1))
                    c.op("act", lambda e: e.activation(out=pT[b][:],
                                                        in_=pv[:, 0:256].rearrange("p (k t) -> p k t", k=2),
                                                        func=AF.Copy), reads=[bdeps[B_TR]], writes=[pT_dep[b]])

                def e_mm():
                    for half in range(2):
                        bk = B_E[half]
                        for kc in range(2):
                            c.op("pe", lambda e: e.matmul(banks[bk][:], lhsT=pT[b][:, kc, :],
                                                          rhs=wpl[:, kc, half * 512:(half + 1) * 512],
                                                          start=(kc == 0), stop=(kc == 1)),
                                 reads=[pT_dep[b], wpl_dep], writes=[bdeps[bk]], sig=(kc == 1))
                ops = [tr_p, e_mm]
                for half in range(2):
                    ops.append(lambda half=half: c.op(
                        "act", lambda e: e.activation(out=t1[:, half * 512:(half + 1) * 512],
                                                      in_=banks[B_E[half]][:], func=AF.Square,
                                                      accum_out=sm[b][:, 3 + half:4 + half]),
                        reads=[bdeps[B_E[half]]], writes=[t1_dep, sm_dep[b][3 + half]]))
                ops.append(lambda: c.op("dve", lambda e: e.tensor_tensor(out=sm[b][:, 5:6], in0=sm[b][:, 3:4],
                                                                          in1=sm[b][:, 4:5], op=ALU.add),
                                        reads=[sm_dep[b][3], sm_dep[b][4]], writes=[sm_dep[b][5]]))
                ops += rsq_ops(sm[b][:, 6:7], sm[b][:, 5:6], sm_dep[b][6], sm_dep[b][5], sm[b][:, 7:8], sm_dep[b][7])
                for half in range(2):
                    ops.append(lambda half=half: c.op(
                        "dve", lambda e: e.scalar_tensor_tensor(out=ee[b][:, half * 512:(half + 1) * 512],
                                                                in0=banks[B_E[half]][:], scalar=sm[b][:, 6:7],
                                                                in1=rowsb[:, 1, half * 512:(half + 1) * 512],
                                                                op0=ALU.mult, op1=ALU.mult),
                        reads=[bdeps[B_E[half]], sm_dep[b][6], rowsb_dep], writes=[ee_dep[b]]))
                return ops

            def TRh(tt):
                b = tt % 2
                pv = bank_bf(B_TR)
                for kc in range(8):
                    c.op("pe", lambda e: e.transpose(out=pv[:, kc * 128:(kc + 1) * 128],
                                                     in_=h1b[:, kc * 128:(kc + 1) * 128], identity=ident[:]),
                         reads=[h1b_dep, ident_dep], writes=[bdeps[B_TR]], sig=(kc == 7))
                c.op("act", lambda e: e.activation(out=h1T[b][:], in_=pv.rearrange("p (k t) -> p k t", k=8),
                                                    func=AF.Copy), reads=[bdeps[B_TR]], writes=[h1T_dep[b]])

            def Bpe(tt):
                b = tt % 2
                for half in range(2):
                    bk = B_GATE[half]
                    for kc in range(8):
                        c.op("pe", lambda e: e.matmul(banks[bk][:], lhsT=h1T[b][:, kc, :],
                                                      rhs=wpg[:, kc, half * 512:(half + 1) * 512],
                                                      start=(kc == 0), stop=(kc == 7)),
                             reads=[h1T_dep[b], wpg_dep], writes=[bdeps[bk]], sig=(kc == 7))

            def B2_ops(tt):
                b = tt % 2
                ops = []
                for half in range(2):
                    ops.append(lambda half=half: c.op(
                        "dve", lambda e: e.tensor_tensor(out=gz[:, half * 512:(half + 1) * 512],
                                                         in0=banks[B_GATE[half]][:],
                                                         in1=rowsb[:, 2, half * 512:(half + 1) * 512], op=ALU.add),
                        reads=[bdeps[B_GATE[half]], rowsb_dep], writes=[gz_dep]))
                ops += [
                    lambda: c.op("act", lambda e: e.activation(out=gz[:], in_=gz[:], func=AF.Exp, scale=-1.0),
                                 reads=[gz_dep], writes=[gz_dep]),
                    lambda: c.op("act", lambda e: e.activation(out=gz[:], in_=gz[:], func=AF.Ln,
                                                                bias=onecol[:, 0:1]),
                                 reads=[gz_dep, onecol_dep], writes=[gz_dep]),
                    lambda: c.op("act", lambda e: e.activation(out=gz[:], in_=gz[:], func=AF.Exp, scale=-1.0),
                                 reads=[gz_dep], writes=[gz_dep]),
                    lambda: c.op("dve", lambda e: e.tensor_tensor(out=gz[:], in0=ee[b][:], in1=gz[:], op=ALU.mult),
                                 reads=[ee_dep[b], gz_dep], writes=[gz_dep]),
                    lambda: c.op("dve", lambda e: e.tensor_tensor(out=ot[b][:], in0=gz[:], in1=h1[b][:], op=ALU.add),
                                 reads=[gz_dep, h1_dep[b]], writes=[ot_dep[b]]),
                    lambda: out_toks.append(c.dma("pool", out[tt * 128:(tt + 1) * 128, :], ot[b][:],
                                                  reads=[ot_dep[b]], slot=ot_dep[b])),
                ]
                return ops

            def interleave(a, b_):
                n_ = max(len(a), len(b_))
                for k_ in range(n_):
                    if k_ < len(a):
                        a[k_]()
                    if k_ < len(b_):
                        b_[k_]()

            load_ylc(0)
            AL(0)
            chain_head(0)
            trp0 = TRp_ops(0)
            interleave(chain_ops(0), trp0[:1])
            TRh(0)
            for op_ in trp0[1:]:
                op_()
            AL(1)
            for tt in range(NT):
                Bpe(tt)
                if tt + 1 < NT:
                    chain_head(tt + 1)
                if tt + 2 < NT:
                    AL(tt + 2)
                trp = TRp_ops(tt + 1) if tt + 1 < NT else []
                seq_b = B2_ops(tt) + trp[:1]
                seq_a = chain_ops(tt + 1) if tt + 1 < NT else []
                interleave(seq_a, seq_b)
                if tt + 1 < NT:
                    TRh(tt + 1)
                for op_ in trp[1:]:
                    op_()
        xs.close()
        E = c.engs["sp"]
        for t in out_toks[-2:]:
            E.wait(t)
        for nm in ("act", "dve", "pool", "pe"):
            pass
    return nc


_NC_CACHE = {}


def _pack_cols(pre_gain, attn_out_gain, lru_out_gain, conv_w, conv_b, b_rgate, b_igate, lru_lambda):
    def colz(v):
        return np.ascontiguousarray(v.reshape(8, 128).T)
    cols = np.zeros((128, NCOLS), np.float32)
    cols[:, C_PRE:C_PRE + 8] = colz(pre_gain)
    cols[:, C_AG:C_AG + 8] = colz(attn_out_gain)
    cols[:, C_LG:C_LG + 8] = colz(lru_out_gain)
    for j in range(4):
        cols[:, C_CW + 8 * j:C_CW + 8 * j + 8] = colz(conv_w[j])
    cols[:, C_CB:C_CB + 8] = colz(conv_b)
    cols[:, C_BR:C_BR + 8] = colz(b_rgate)
    cols[:, C_BI:C_BI + 8] = colz(b_igate)
    cols[:, C_LAM:C_LAM + 8] = colz(lru_lambda)
    return cols


def kernel(x, p, w_in, b_f, pre_gain, post_gain, conv_w, conv_b, w_rgate, b_rgate,
           w_igate, b_igate, lru_lambda, attn_out_gain, lru_out_gain, w_out,
           w_ple, ple_gain, w_ple_gate, b_ple_gate):
    f = lambda a: np.ascontiguousarray(np.asarray(a, dtype=np.float32))
    x = f(x); p = f(p)
    cols = _pack_cols(f(pre_gain)[0], f(attn_out_gain)[0], f(lru_out_gain)[0], f(conv_w)[0], f(conv_b)[0],
                      f(b_rgate)[0], f(b_igate)[0], f(lru_lambda)[0])
    rows = np.ascontiguousarray(np.stack([f(post_gain)[0], f(ple_gain)[0], f(b_ple_gate)[0]], axis=0))
    bf = np.ascontiguousarray(f(b_f)[0].reshape(8, 1))
    shared = {"w_in": f(w_in)[0], "w_out": f(w_out)[0], "w_ple": f(w_ple)[0], "w_pg": f(w_ple_gate)[0],
              "w_r": f(w_rgate)[0], "w_i": f(w_igate)[0], "cols": cols, "rows": rows, "bf": bf}
    if "nc" not in _NC_CACHE:
        _NC_CACHE["nc"] = build_nc()
    nc = _NC_CACHE["nc"]
    in_maps = []
    for b in range(8):
        m = dict(shared)
        m["x"] = x[b]
        m["p"] = p[0, b]
        in_maps.append(m)
    res = run_bass_kernel_spmd(nc, in_maps, core_ids=list(range(8)))
    return np.stack([np.asarray(r["out"], dtype=np.float32) for r in res.results], axis=0)
```
